# Optimizing a Trainium2 kernel written in Bass

```python
import math
import jax
import jax.numpy as jnp
from jax import lax
import numpy as np

D_MODEL = 1024
BATCH = 32
SEQ = 256
DEPTH = 2
DEC_BATCH = 2
DEC_SEQ = 2048
PAST_LEN = 512

GRID_W = 64
MIX_WIDTH = D_MODEL
A_WIDTH = MIX_WIDTH // 2
A_HEADS = 4
A_DV = A_WIDTH // A_HEADS
A_DK = A_DV
QK = A_HEADS * A_DK
CONV_K = 3
DELTA_CHUNK = 64
B_WIDTH = MIX_WIDTH // 4
B_GROUPS = 4
B_GC = B_WIDTH // B_GROUPS
SGU_CHUNK = 128
C_WIDTH = MIX_WIDTH - A_WIDTH - B_WIDTH
POOL_WINDOWS = (2, 4, 8, 16)
C_GC = C_WIDTH // len(POOL_WINDOWS)
IN_SPLITS = (QK, QK, A_WIDTH, A_WIDTH, 2 * A_HEADS, 2 * A_HEADS, B_WIDTH, B_WIDTH, C_WIDTH)
IN_COLS = sum(IN_SPLITS)
FF_HIDDEN = ((8 * D_MODEL // 3 + 255) // 256) * 256
N_MOD = 6
EPS = 1e-6

kernel_name = 'hybrid_delta_sgu_pool_dit_step'


def rmsnorm(x, g):
    xf = x.astype(jnp.float32)
    y = xf * lax.rsqrt(jnp.mean(xf * xf, axis=-1, keepdims=True) + EPS)
    return (y * g.astype(jnp.float32)).astype(x.dtype)


def l2norm(x):
    xf = x.astype(jnp.float32)
    return xf * lax.rsqrt(jnp.sum(xf * xf, axis=-1, keepdims=True) + EPS)


def short_conv(x, w):
    k = w.shape[0]
    pad = k // 2
    return lax.conv_general_dilated(
        x, w[:, None, :].astype(x.dtype), window_strides=(1,), padding=[(pad, k - 1 - pad)],
        dimension_numbers=('NWC', 'WIO', 'NWC'), feature_group_count=x.shape[-1])


def gated_delta_chunked(q, k, v, g, beta, s0):
    bsz, n, h, _ = q.shape
    dv = v.shape[-1]
    nc = n // DELTA_CHUNK
    cs = DELTA_CHUNK

    def chunks(t):
        t = t.reshape((bsz, nc, cs, h) + t.shape[3:])
        return jnp.moveaxis(jnp.moveaxis(t, 1, 0), 3, 2)

    q, k, v, g, beta = chunks(q), chunks(k), chunks(v), chunks(g), chunks(beta)
    gc = jnp.cumsum(g, axis=-1)
    idx = jnp.arange(cs)
    incl = idx[:, None] >= idx[None, :]
    strict = idx[:, None] > idx[None, :]
    decay = jnp.exp(jnp.where(incl, gc[..., :, None] - gc[..., None, :], -jnp.inf))
    kk = jnp.einsum('nbhid,nbhjd->nbhij', k, k)
    lower = jnp.where(strict, beta[..., :, None] * kk * decay, 0.0)
    eye = jnp.eye(cs, dtype=jnp.float32)
    t_mat = lax.linalg.triangular_solve(eye + lower, jnp.broadcast_to(eye, lower.shape),
                                        left_side=True, lower=True)
    u = jnp.einsum('nbhij,nbhjd->nbhid', t_mat, v * beta[..., None])
    w = jnp.einsum('nbhij,nbhjd->nbhid', t_mat, k * (beta * jnp.exp(gc))[..., None])
    qk = jnp.einsum('nbhid,nbhjd->nbhij', q, k) * decay
    q_dec = q * jnp.exp(gc)[..., None]
    k_dec = k * jnp.exp(gc[..., -1:] - gc)[..., None]
    g_tot = jnp.exp(gc[..., -1])

    def step(s, inp):
        qk_c, qd_c, kd_c, u_c, w_c, gt_c = inp
        v_new = u_c - jnp.einsum('bhid,bhde->bhie', w_c, s)
        o = jnp.einsum('bhid,bhde->bhie', qd_c, s) + jnp.einsum('bhij,bhje->bhie', qk_c, v_new)
        s = s * gt_c[..., None, None] + jnp.einsum('bhid,bhie->bhde', kd_c, v_new)
        return s, o

    s_fin, o = lax.scan(step, s0, (qk, q_dec, k_dec, u, w, g_tot))
    o = jnp.moveaxis(jnp.moveaxis(o, 2, 3), 0, 1).reshape(bsz, n, h, dv)
    return o, s_fin


def delta_mixer(q, k, v, z, beta_logit, alpha_logit, conv_w, a_log, dt_bias, norm_g, s0):
    bsz, n, _ = q.shape
    qkv = jax.nn.silu(short_conv(jnp.concatenate([q, k, v], axis=-1), conv_w))
    q, k, v = jnp.split(qkv, [QK, 2 * QK], axis=-1)
    q = l2norm(q.reshape(bsz, n, A_HEADS, A_DK)) * (A_DK ** -0.5)
    k = l2norm(k.reshape(bsz, n, A_HEADS, A_DK))
    v = v.reshape(bsz, n, A_HEADS, A_DV).astype(jnp.float32)
    outs = []
    states = []
    for d in range(2):
        hs = slice(d * A_HEADS, (d + 1) * A_HEADS)
        beta = jax.nn.sigmoid(beta_logit[..., hs].astype(jnp.float32))
        g = -jnp.exp(a_log[d].astype(jnp.float32)) * jax.nn.softplus(
            alpha_logit[..., hs].astype(jnp.float32) + dt_bias[d].astype(jnp.float32))
        args = (q, k, v, g, beta)
        if d == 1:
            args = tuple(jnp.flip(t, axis=1) for t in args)
        o, s = gated_delta_chunked(*args, s0[:, d].astype(jnp.float32))
        if d == 1:
            o = jnp.flip(o, axis=1)
        outs.append(o)
        states.append(s)
    o = outs[0] + outs[1]
    o = rmsnorm(o, norm_g) * jax.nn.silu(z.reshape(bsz, n, A_HEADS, A_DV).astype(jnp.float32))
    return o.reshape(bsz, n, A_WIDTH).astype(z.dtype), jnp.stack(states, axis=1)


def sgu_mixer(u, v, norm_g, w_s, b_s):
    bsz, n, _ = u.shape
    nch = n // SGU_CHUNK
    u = jax.nn.gelu(u)
    v = jax.nn.gelu(v).reshape(bsz, nch, SGU_CHUNK, B_GROUPS, B_GC)
    v = rmsnorm(v, norm_g.reshape(B_GROUPS, B_GC))
    s = jnp.einsum('gts,bnsgc->bntgc', w_s, v) + b_s.T[None, None, :, :, None]
    return u * s.reshape(bsz, n, B_WIDTH)


def pool_mixer(p, w_pool, scale, seg_len):
    bsz, n, _ = p.shape
    ps = p.reshape(bsz * (n // seg_len), seg_len, C_WIDTH).astype(jnp.float32)
    pos = jnp.arange(seg_len)
    outs = []
    for i, win in enumerate(POOL_WINDOWS):
        x = ps[..., i * C_GC:(i + 1) * C_GC]
        csum = jnp.pad(jnp.cumsum(x, axis=1), ((0, 0), (1, 0), (0, 0)))
        lo = jnp.clip(pos - win // 2, 0, seg_len)
        hi = jnp.clip(pos + win - win // 2, 0, seg_len)
        mean = (jnp.take(csum, hi, axis=1) - jnp.take(csum, lo, axis=1)) / \
            (hi - lo).astype(jnp.float32)[None, :, None]
        outs.append(jnp.einsum('bnc,cd->bnd', mean - x, w_pool[i].astype(jnp.float32)))
    y = jnp.concatenate(outs, axis=-1).reshape(bsz, n, C_WIDTH) * scale.astype(jnp.float32)
    return y.astype(p.dtype)


def grid_pos_embed(rows, d):
    quarter = d // 4
    omega = 1.0 / (10000.0 ** (jnp.arange(quarter, dtype=jnp.float32) / quarter))
    r = jnp.arange(rows, dtype=jnp.float32)[:, None] * omega
    cl = jnp.arange(GRID_W, dtype=jnp.float32)[:, None] * omega
    row_emb = jnp.concatenate([jnp.sin(r), jnp.cos(r)], axis=-1)
    col_emb = jnp.concatenate([jnp.sin(cl), jnp.cos(cl)], axis=-1)
    emb = jnp.concatenate([jnp.broadcast_to(row_emb[:, None, :], (rows, GRID_W, d // 2)),
                           jnp.broadcast_to(col_emb[None, :, :], (rows, GRID_W, d // 2))], axis=-1)
    return emb.reshape(rows * GRID_W, d)


def trunk_layer(x, mod, s0, prm, l, pool_seg):
    sh1, sc1, g1, sh2, sc2, g2 = jnp.split(mod, N_MOD, axis=-1)
    h = rmsnorm(x, prm['norm1_g'][l]) * (1 + sc1) + sh1
    cols = jnp.einsum('bnd,de->bne', h, prm['w_in'][l])
    split_idx = [int(i) for i in np.cumsum(IN_SPLITS)[:-1]]
    q, k, v, z, bl, al, u, vb, pp = jnp.split(cols, split_idx, axis=-1)
    ya, s_fin = delta_mixer(q, k, v, z, bl, al, prm['conv_w'][l], prm['a_log'][l],
                            prm['dt_bias'][l], prm['delta_norm_g'][l], s0)
    yb = sgu_mixer(u, vb, prm['sgu_norm_g'][l], prm['w_spatial'][l], prm['b_spatial'][l])
    yc = pool_mixer(pp, prm['w_pool'][l], prm['pool_scale'][l], pool_seg)
    mix = jnp.einsum('bne,ed->bnd', jnp.concatenate([ya, yb, yc], axis=-1), prm['w_out'][l])
    x = x + g1 * mix
    h = rmsnorm(x, prm['norm2_g'][l]) * (1 + sc2) + sh2
    gate, up = jnp.split(jnp.einsum('bnd,df->bnf', h, prm['w_gu'][l]), 2, axis=-1)
    x = x + g2 * jnp.einsum('bnf,fd->bnd', jax.nn.silu(gate) * up, prm['w_down'][l])
    return x, s_fin


def setup_inputs(seed: int = 0) -> dict:
    key = jax.random.key(seed)
    ks = jax.random.split(key, 24)
    f32 = jnp.float32

    def nrm(k, shape, scale):
        return jax.random.normal(k, shape, f32) * scale

    dt = jnp.exp(jax.random.uniform(ks[8], (DEPTH, 2, A_HEADS), f32, math.log(1e-3), math.log(1e-1)))
    return {
        'x_prompt': nrm(ks[0], (BATCH, SEQ, D_MODEL), 1.0),
        'x_sample': nrm(ks[1], (DEC_BATCH, DEC_SEQ, D_MODEL), 1.0),
        'state_delta': nrm(ks[2], (DEC_BATCH, DEPTH, 2, A_HEADS, A_DK, A_DV), 0.1),
        'c': nrm(ks[3], (DEC_BATCH, D_MODEL), 1.0),
        'c_ctx': nrm(ks[4], (D_MODEL,), 1.0),
        'w_in': nrm(ks[5], (DEPTH, D_MODEL, IN_COLS), D_MODEL ** -0.5),
        'conv_w': nrm(ks[6], (DEPTH, CONV_K, 2 * QK + A_WIDTH), CONV_K ** -0.5),
        'a_log': jnp.log(jax.random.uniform(ks[7], (DEPTH, 2, A_HEADS), f32, 1.0, 16.0)),
        'dt_bias': dt + jnp.log(-jnp.expm1(-dt)),
        'delta_norm_g': 1.0 + nrm(ks[9], (DEPTH, A_DV), 0.02),
        'sgu_norm_g': 1.0 + nrm(ks[10], (DEPTH, B_WIDTH), 0.02),
        'w_spatial': nrm(ks[11], (DEPTH, B_GROUPS, SGU_CHUNK, SGU_CHUNK), SGU_CHUNK ** -0.5),
        'b_spatial': 1.0 + nrm(ks[12], (DEPTH, B_GROUPS, SGU_CHUNK), 0.02),
        'w_pool': nrm(ks[13], (DEPTH, len(POOL_WINDOWS), C_GC, C_GC), C_GC ** -0.5),
        'pool_scale': 1.0 + nrm(ks[14], (DEPTH, C_WIDTH), 0.02),
        'w_out': nrm(ks[15], (DEPTH, MIX_WIDTH, D_MODEL), MIX_WIDTH ** -0.5),
        'norm1_g': 1.0 + nrm(ks[16], (DEPTH, D_MODEL), 0.02),
        'norm2_g': 1.0 + nrm(ks[17], (DEPTH, D_MODEL), 0.02),
        'w_mod': nrm(ks[18], (DEPTH, D_MODEL, N_MOD * D_MODEL), 0.5 * D_MODEL ** -0.5),
        'b_mod': nrm(ks[19], (DEPTH, N_MOD * D_MODEL), 0.02),
        'w_gu': nrm(ks[20], (DEPTH, D_MODEL, 2 * FF_HIDDEN), D_MODEL ** -0.5),
        'w_down': nrm(ks[21], (DEPTH, FF_HIDDEN, D_MODEL), FF_HIDDEN ** -0.5),
        'norm_f': 1.0 + nrm(ks[22], (D_MODEL,), 0.02),
    }


def reference(x_prompt, x_sample, state_delta, c, c_ctx, w_in, conv_w, a_log, dt_bias,
              delta_norm_g, sgu_norm_g, w_spatial, b_spatial, w_pool, pool_scale, w_out,
              norm1_g, norm2_g, w_mod, b_mod, w_gu, w_down, norm_f):
    prm = {'w_in': w_in, 'conv_w': conv_w, 'a_log': a_log, 'dt_bias': dt_bias,
           'delta_norm_g': delta_norm_g, 'sgu_norm_g': sgu_norm_g, 'w_spatial': w_spatial,
           'b_spatial': b_spatial, 'w_pool': w_pool, 'pool_scale': pool_scale, 'w_out': w_out,
           'norm1_g': norm1_g, 'norm2_g': norm2_g, 'w_gu': w_gu, 'w_down': w_down}

    xc = x_prompt
    zero_state = jnp.zeros((x_prompt.shape[0], 2, A_HEADS, A_DK, A_DV), jnp.float32)
    ctx_states = []
    for l in range(DEPTH):
        mod_ctx = (jnp.einsum('d,de->e', jax.nn.silu(c_ctx), w_mod[l]) + b_mod[l])[None, None, :]
        xc, s_ctx = trunk_layer(xc, mod_ctx, zero_state, prm, l, xc.shape[1])
        ctx_states.append(s_ctx)
    y_prompt = rmsnorm(xc, norm_f)
    new_state_delta = jnp.stack(ctx_states, axis=1).astype(x_prompt.dtype)

    rows = x_sample.shape[1] // GRID_W
    xs = x_sample + grid_pos_embed(rows, D_MODEL).astype(x_sample.dtype)[None]
    for l in range(DEPTH):
        mod_lat = (jnp.einsum('bd,de->be', jax.nn.silu(c), w_mod[l]) + b_mod[l])[:, None, :]
        xs, _ = trunk_layer(xs, mod_lat, state_delta[:, l], prm, l, GRID_W)
    y_sample = rmsnorm(xs, norm_f)

    return (y_prompt, y_sample, new_state_delta)
```

```python
import contextlib
import math
import numpy as np
import concourse.bass as bass
import concourse.mybir as mybir
from concourse.bass_utils import run_bass_kernel_spmd

F32 = mybir.dt.float32
BF16 = mybir.dt.bfloat16
AF = mybir.ActivationFunctionType
ALU = mybir.AluOpType

D = 1024
DC = 8
NCOL = 2832
FF = 2816
FC = 22
EPS = 1e-6
NEG = -30000.0
STRICT = True
INTERLEAVE = True
SKIPB = False
OVERLAP = True
BLIMIT = 10 ** 9
C_Q, C_K, C_V, C_Z, C_B, C_A, C_U, C_VB, C_P = 0, 512, 1024, 1536, 2048, 2056, 2064, 2320, 2576


class Ev:
    __slots__ = ("sem", "sid", "val", "eng")

    def __init__(self, sem, sid, val, eng):
        self.sem, self.sid, self.val, self.eng = sem, sid, val, eng


class TK:
    ROLL = 30000
    NDMA = 6

    def __init__(self, nc, stack):
        self.nc = nc
        self.stack = stack
        self.engs = {"pe": nc.tensor, "act": nc.scalar, "dve": nc.vector, "pool": nc.gpsimd, "sp": nc.sync}
        self.sem = {}
        self.cnt = {}
        self.nsem = 0
        for e in self.engs:
            self._newsem(e)
        self.waited = {e: {} for e in self.engs}
        self.lastw = {}
        self.rd = {}
        self.dpool = {}
        self.dn = {}
        self.last = {}
        self.nwait = 0
        self.nins = 0

    def _alloc(self, name):
        self.nsem += 1
        return self.stack.enter_context(self.nc.semaphore("%s_%d" % (name, self.nsem)))

    def _newsem(self, e):
        self.sem[e] = (self._alloc("e" + e), self.nsem)
        self.cnt[e] = 0

    def _need(self, eng, reads, writes, sreads=()):
        need = {}

        def add(ev, force=False):
            if ev is None or (ev.eng == eng and (eng == "pe" or not STRICT) and not force):
                return
            cur = need.get(ev.sid)
            if cur is None or cur.val < ev.val:
                need[ev.sid] = ev
        for k in reads:
            add(self.lastw.get(k))
            if isinstance(k, str) and k.startswith("ps"):
                for ev in self.rd.get(k, {}).values():
                    add(ev)
        for k in sreads:
            add(self.lastw.get(k), True)
        for k in writes:
            add(self.lastw.get(k))
            for ev in self.rd.get(k, {}).values():
                add(ev)
        return need

    def _dowaits(self, eng, need):
        w = self.waited[eng]
        for sid, ev in need.items():
            if w.get(sid, 0) >= ev.val:
                continue
            self.engs[eng].wait_ge(ev.sem, ev.val)
            w[sid] = ev.val
            self.nwait += 1

    def _record(self, ev, reads, writes):
        for k in writes:
            self.lastw[k] = ev
            self.rd[k] = {}
        for k in reads:
            self.rd.setdefault(k, {})[ev.sid] = ev

    def op(self, eng, fn, reads=(), writes=(), carrier=None, sreads=()):
        n0 = self.nwait
        self._dowaits(eng, self._need(eng, reads, writes, sreads))
        if carrier is not None and self.nwait > n0:
            carrier()
        ins = fn()
        if isinstance(ins, (list, tuple)):
            ins = ins[-1]
        if self.cnt[eng] >= self.ROLL:
            self._newsem(eng)
        s, sid = self.sem[eng]
        self.cnt[eng] += 1
        ins.then_inc(s, 1)
        ev = Ev(s, sid, self.cnt[eng], eng)
        self.last[eng] = ev
        self._record(ev, list(reads) + list(sreads), writes)
        self.nins += 1
        return ev

    def dma(self, q, fn, reads=(), writes=()):
        need = self._need("dma", reads, writes)
        if q not in self.dpool:
            self.dpool[q] = [[self._alloc("d" + q), self.nsem, 0] for _ in range(self.NDMA)]
            self.dn[q] = 0
        slot = self.dpool[q][self.dn[q] % self.NDMA]
        self.dn[q] += 1
        if slot[2] > 0:
            ev0 = Ev(slot[0], slot[1], slot[2], "dma")
            cur = need.get(ev0.sid)
            if cur is None or cur.val < ev0.val:
                need[ev0.sid] = ev0
        self._dowaits(q, need)
        ins = fn()
        slot[2] += 16
        ins.then_inc(slot[0], 16)
        ev = Ev(slot[0], slot[1], slot[2], "dma")
        self._record(ev, reads, writes)
        return ev

    def wait_all(self, eng):
        need = {}
        for e in self.engs:
            if e == eng or e not in self.last:
                continue
            ev = self.last[e]
            need[ev.sid] = ev
        for q, slots in self.dpool.items():
            for s, sid, val in slots:
                if val > 0:
                    need[sid] = Ev(s, sid, val, "dma")
        self._dowaits(eng, need)

    def barrier(self):
        for e in self.engs:
            self.wait_all(e)
        self.lastw = {}
        self.rd = {}


def bcast(ap, pos, n):
    l = [list(x) for x in ap.ap]
    l.insert(pos + 1, [0, n])
    return bass.AP(tensor=ap.tensor, offset=ap.offset, ap=l)


class PSA:
    def __init__(self):
        self.free = list(range(8))

    def get(self, n=1):
        if n == 1:
            return self.free.pop(0)
        for b in self.free:
            if b % 2 == 0 and (b + 1) in self.free:
                self.free.remove(b)
                self.free.remove(b + 1)
                return b
        raise RuntimeError("no psum pair")

    def put(self, b, n=1):
        for i in range(n):
            self.free.append(b + i)


def build(T=2048, NL=2, dbg=None):
    NS, NT, NG = T // 256, T // 128, T // 512
    HT = min(1024, T)
    NHALF = T // HT
    NGH = HT // 512
    nc = bass.Bass("TRN2", target_bir_lowering=False)

    def din(name, shape):
        return nc.dram_tensor(name, list(shape), F32, kind="ExternalInput").ap()

    xin = din("xin", [T, D])
    omega_in = din("omega", [128, 2])
    cvec = din("cvec", [128, 8])
    s0in = din("s0in", [NL, 2, 128, 4, 128])
    chain_in = din("chain", [128, 1])
    poolm_in = din("poolm", [128, 4, 4, 128])
    w_mod = din("w_mod", [NL, D, 6 * D])
    b_mod = din("b_mod", [NL, 128, 48])
    w_in = din("w_in", [NL, D, NCOL])
    conv_w = din("conv_w", [NL, 128, 12, 3])
    a_log = din("a_log", [NL, 128, 8])
    dt_bias = din("dt_bias", [NL, 128, 8])
    dng = din("dng", [NL, 128, 512])
    sng = din("sng", [NL, 128, 256])
    wsT = din("wsT", [NL, 128, 4, 128])
    bsp = din("bsp", [NL, 1, 512])
    wpool = din("wpool", [NL, 128, 2, 64])
    pscale = din("pscale", [NL, 128, 2])
    w_out = din("w_out", [NL, D, D])
    n1g = din("n1g", [NL, 128, 8])
    n2g = din("n2g", [NL, 128, 8])
    nfg = din("nfg", [128, 8])
    w_gu = din("w_gu", [NL, D, 2 * FF])
    w_down = din("w_down", [NL, FF, D])
    cmf_in = din("cmf", [128, 6, 128])
    cmb_in = din("cmb", [128, 9, 128])
    yout = nc.dram_tensor("yout", [T, D], F32, kind="ExternalOutput").ap()
    stout = nc.dram_tensor("stout", [NS, NL, 2, 4, 128, 128], F32, kind="ExternalOutput").ap()
    dbg_out = {}
    if dbg:
        for name, shape in dbg.items():
            dbg_out[name] = nc.dram_tensor("dbg_" + name, list(shape), F32, kind="ExternalOutput").ap()

    with contextlib.ExitStack() as st:
        tk = TK(nc, st)
        psa = PSA()

        def sb(name, shape, dt):
            return st.enter_context(nc.sbuf_tensor(name, list(shape), dt))

        PS = st.enter_context(nc.psum_tensor("PS", [128, 8, 512], F32))

        def pk(b):
            return "ps%d" % b

        V, Sc, Pl, Pe = nc.vector, nc.scalar, nc.gpsimd, nc.tensor

        xT = sb("xT", [128, DC, T], F32)
        cm = sb("cm_sb", [128, 6, 128], F32)
        cmb = sb("cmb_sb", [128, 9, 128], BF16)
        onesb = sb("onesb", [128, 128], BF16)
        onesf = sb("onesf", [128, 128], F32)
        chain = sb("chain_sb", [128, 1], F32)
        csil = sb("csil", [128, 8], F32)
        modv = sb("modv", [128, NL, 48], F32)
        gm1 = sb("gm1", [128, NL, 8], F32)
        gm2 = sb("gm2", [128, NL, 8], F32)
        nfg_sb = sb("nfg_sb", [128, 8], F32)
        epst = sb("epst", [128, 4], F32)
        lsm = sb("lsm", [128, 64], F32)
        convw = sb("convw", [128, 12, 3], F32)
        bmod_sb = sb("bmod_sb", [128, 48], F32)
        NAR = (132096 + 7168) // 2
        arena = sb("arena", [128, NAR], BF16)

        def carve(off, shape, dt):
            n = int(np.prod(shape))
            nb = n * (4 if dt == F32 else 2)
            assert off % 4 == 0 and off + nb <= NAR * 2, (off, nb, NAR * 2)
            a = arena[:, off // 2: (off + nb) // 2]
            if dt == F32:
                a = a.bitcast(F32)
            if len(shape) == 2:
                return a.rearrange("p (a b) -> p a b", a=shape[0]), off + nb
            if len(shape) == 3:
                return a.rearrange("p (a b c) -> p a b c", a=shape[0], b=shape[1]), off + nb
            return a, off + nb

        IDENT, TRI_F, TRI_B, BLK, NEG_F, NEG_B = 0, 1, 2, 3, 4, 5
        STR_F, LM0 = 1, 3

        tk.dma("sp", lambda: nc.sync.dma_start(out=cm[:], in_=cmf_in), writes=["cm"])
        tk.dma("pool", lambda: Pl.dma_start(out=cmb[:], in_=cmb_in), writes=["cmb"])
        tk.dma("sp", lambda: nc.sync.dma_start(out=chain[:], in_=chain_in), writes=["chain"])
        tk.dma("sp", lambda: nc.sync.dma_start(out=csil[:], in_=cvec), writes=["csil"])
        tk.dma("sp", lambda: nc.sync.dma_start(out=nfg_sb[:], in_=nfg), writes=["nfg"])
        tk.op("dve", lambda: V.memset(onesf[:], 1.0), writes=["onesf"])
        tk.op("dve", lambda: V.memset(onesb[:], 1.0), writes=["onesb"])
        tk.op("dve", lambda: V.memset(epst[:, 0:1], EPS), writes=["epst"])
        tk.op("dve", lambda: V.memset(epst[:, 1:2], 128.0 * EPS), writes=["epst"])
        tk.op("dve", lambda: V.memset(epst[:, 2:3], 1.0), writes=["epst"])
        tk.op("dve", lambda: V.memset(epst[:, 3:4], 0.0), writes=["epst"])
        tk.op("act", lambda: Sc.activation(out=csil[:], in_=csil[:], func=AF.Silu), reads=["csil"], writes=["csil"])
        junk = sb("junk", [128, 4], F32)
        tk.op("act", lambda: Sc.copy(out=junk[:, 0:2], in_=epst[:, 0:2]), reads=["epst"])
        actc = lambda: Sc.copy(out=junk[:, 2:3], in_=junk[:, 0:1])
        identf = cm[:, IDENT, :]
        identb = cmb[:, IDENT, :]

        off = 0
        xtk, off = carve(off, [2, 1, D], F32)
        for t in range(NT):
            i = t % 2
            tk.dma("sp", lambda t=t, i=i: nc.sync.dma_start(out=xtk[:, i, 0, :], in_=xin[t * 128:(t + 1) * 128, :]), writes=[("xtk", i)])
            for hh in range(2):
                b = psa.get()
                tk.op("pe", lambda i=i, hh=hh, b=b: [Pe.transpose(out=PS[:, b, j * 128:(j + 1) * 128], in_=xtk[:, i, 0, (hh * 4 + j) * 128:(hh * 4 + j + 1) * 128], identity=identf) for j in range(4)],
                      reads=[("xtk", i), "cm"], writes=[pk(b)])
                eng = "act" if hh == 0 else "dve"
                if eng == "act":
                    tk.op("act", lambda t=t, hh=hh, b=b: Sc.copy(out=xT[:, hh * 4:(hh + 1) * 4, t * 128:(t + 1) * 128], in_=PS[:, b, :].rearrange("p (j n) -> p j n", j=4)),
                          reads=[pk(b)], writes=[("xT", t // 4)])
                else:
                    tk.op("dve", lambda t=t, hh=hh, b=b: V.tensor_copy(out=xT[:, hh * 4:(hh + 1) * 4, t * 128:(t + 1) * 128], in_=PS[:, b, :].rearrange("p (j n) -> p j n", j=4)),
                          reads=[pk(b)], writes=[("xT", t // 4)])
                psa.put(b)

        I32 = mybir.dt.int32
        NR = T // 64
        omg, off = carve(off, [1, 2], F32)
        posr, off = carve(off, [1, 64], F32)
        ptab, off = carve(off, [DC, 64], F32)
        pki_, off = carve(off, [DC, 64], F32)
        pki = pki_[:, :, :].rearrange("p a n -> p (a n)").bitcast(I32).rearrange("p (a n) -> p a n", a=DC)
        tk.dma("sp", lambda: nc.sync.dma_start(out=omg[:, 0, :], in_=omega_in), writes=["omg"])
        tk.op("pool", lambda: Pl.iota(posr[:, 0, :], pattern=[[1, 64]], base=0, channel_multiplier=0, allow_small_or_imprecise_dtypes=True), writes=["posr"])
        for c in range(DC):
            ph = 0.0 if (c % 4) < 2 else math.pi / 2
            tk.op("dve", lambda c=c, ph=ph: V.tensor_scalar(out=ptab[:, c, :], in0=posr[:, 0, :], scalar1=omg[:, 0, c % 2:c % 2 + 1], scalar2=ph, op0=ALU.mult, op1=ALU.add),
                  reads=["posr"], sreads=["omg"], writes=["ptab"])
        tk.op("dve", lambda: V.tensor_scalar(out=pki, in0=ptab[:, :, :], scalar1=1.0 / (2 * math.pi), scalar2=None, op0=ALU.mult), reads=["ptab"], writes=["pki"])
        tk.op("dve", lambda: V.scalar_tensor_tensor(out=ptab[:, :, :], in0=pki, scalar=-2 * math.pi, in1=ptab[:, :, :], op0=ALU.mult, op1=ALU.add), reads=["pki", "ptab"], writes=["ptab"])
        tk.op("act", lambda: Sc.activation(out=ptab[:, :, :], in_=ptab[:, :, :], func=AF.Sin), reads=["ptab"], writes=["ptab"])
        for g in range(NG):
            r0 = g * 8
            for c in range(DC):
                if c < 4:
                    src = bcast(ptab[:, c, r0:r0 + 8], 1, 64)
                else:
                    src = bcast(ptab[:, c, 0:64], 0, 8)
                tk.op("dve", lambda c=c, src=src, g=g: V.scalar_tensor_tensor(out=xT[:, c, g * 512:(g + 1) * 512].rearrange("p (r n) -> p r n", n=64), in0=src, scalar=chain[:, 0:1],
                                                                           in1=xT[:, c, g * 512:(g + 1) * 512].rearrange("p (r n) -> p r n", n=64), op0=ALU.mult, op1=ALU.add),
                      reads=["ptab", ("xT", g)], sreads=["chain"], writes=[("xT", g)])

        wmq, off = carve(off, [4 * 8, 512], BF16)
        modrow, off = carve(off, [1, 6 * D], F32)
        csb, off = carve(off, [1, 8], BF16)
        tk.op("dve", lambda: V.tensor_copy(out=csb[:, 0, :], in_=csil[:]), reads=["csil"], writes=["csb"])
        for l in range(NL):
            tk.dma("sp", lambda l=l: nc.sync.dma_start(out=bmod_sb[:], in_=b_mod[l]), writes=["bmod"])
            for cb in range(12):
                i = cb % 4
                tk.dma("pool", lambda l=l, cb=cb, i=i: Pl.dma_start(out=wmq[:, i * 8:(i + 1) * 8, :], in_=w_mod[l, :, cb * 512:(cb + 1) * 512].rearrange("(kc p) n -> p kc n", p=128)),
                       writes=[("wmq", i)])
                b = psa.get()
                tk.op("pe", lambda i=i, b=b: [Pe.matmul(PS[0:1, b, :], lhsT=csb[:, 0, kc:kc + 1], rhs=wmq[:, i * 8 + kc, :], start=(kc == 0), stop=(kc == 7)) for kc in range(8)],
                      reads=[("wmq", i), "csb"], writes=[pk(b)])
                tk.op("act", lambda cb=cb, b=b: Sc.copy(out=modrow[0:1, 0, cb * 512:(cb + 1) * 512], in_=PS[0:1, b, :]), reads=[pk(b)], writes=["modrow"])
                psa.put(b)
            b = psa.get()
            tk.op("pe", lambda b=b: [Pe.matmul(PS[:, b, j:j + 1], lhsT=modrow[0:1, 0, j * 128:(j + 1) * 128], rhs=onesf[0:1, 0:1], start=True, stop=True) for j in range(48)],
                  reads=["modrow", "onesf"], writes=[pk(b)])
            tk.op("dve", lambda l=l, b=b: V.tensor_tensor(out=modv[:, l, :], in0=PS[:, b, 0:48], in1=bmod_sb[:], op=ALU.add), reads=[pk(b), "bmod"], writes=["modv"])
            psa.put(b)
            tk.dma("sp", lambda l=l: nc.sync.dma_start(out=gm1[:, l, :], in_=n1g[l]), writes=["gm1"])
            tk.dma("sp", lambda l=l: nc.sync.dma_start(out=gm2[:, l, :], in_=n2g[l]), writes=["gm2"])
            tk.op("dve", lambda l=l: V.scalar_tensor_tensor(out=gm1[:, l, :], in0=modv[:, l, 8:16], scalar=1.0, in1=gm1[:, l, :], op0=ALU.add, op1=ALU.mult), reads=["modv", "gm1"], writes=["gm1"])
            tk.op("dve", lambda l=l: V.scalar_tensor_tensor(out=gm2[:, l, :], in0=modv[:, l, 32:40], scalar=1.0, in1=gm2[:, l, :], op0=ALU.add, op1=ALU.mult), reads=["modv", "gm2"], writes=["gm2"])
        tk.barrier()

        def rsqrt_psum(b, rs, rinv, scale, eps_ap):
            tk.op("act", lambda: Sc.activation(out=rs[:, 0, :], in_=PS[:, b, :], func=AF.Ln, bias=eps_ap, scale=scale), reads=[pk(b), "epst"], writes=["rs"])
            tk.op("act", lambda: Sc.activation(out=rinv[:, 0, :], in_=rs[:, 0, :], func=AF.Exp, scale=-0.5), reads=["rs"], writes=["rinv"])

        def rms_rinv_gen(cols, nfeat_scale, sqb, rs, rinv, xkeys):
            b = psa.get()
            for c in range(DC):
                i = c % 2
                tk.op("act", lambda c=c, i=i: Sc.activation(out=sqb[:, i, :], in_=xT[:, c, cols], func=AF.Square), reads=xkeys, writes=[("sqb", i)])
                tk.op("pe", lambda c=c, i=i, b=b: Pe.matmul(PS[:, b, :], lhsT=onesb[:], rhs=sqb[:, i, :], start=(c == 0), stop=(c == DC - 1)), reads=[("sqb", i), "onesb"], writes=[pk(b)])
                yield
            rsqrt_psum(b, rs, rinv, nfeat_scale, epst[:, 0:1])
            psa.put(b)
            yield

        def rms_rinv(*a):
            for _ in rms_rinv_gen(*a):
                pass

        def norm_mod_gen(cols, gmv, shv, hT, hcols, hkey, sqb, rs, rinv, tmp, xkeys):
            yield from rms_rinv_gen(cols, 1.0 / D, sqb, rs, rinv, xkeys)
            for c in range(DC):
                i = c % 2
                tk.op("dve", lambda c=c, i=i: V.tensor_tensor(out=tmp[:, i, :], in0=xT[:, c, cols], in1=rinv[:, 0, :], op=ALU.mult), reads=xkeys + ["rinv"], writes=[("tmp", i)])
                tk.op("act", lambda c=c, i=i: Sc.activation(out=hT[:, c, hcols], in_=tmp[:, i, :], func=AF.Identity, bias=shv[:, c:c + 1], scale=gmv[:, c:c + 1]),
                      reads=[("tmp", i), "modv", "gm1", "gm2"], writes=[hkey])
                yield

        def norm_mod(*a):
            for _ in norm_mod_gen(*a):
                pass

        def interleave(*gens):
            gens = [g for g in gens if g is not None]
            alive = [True] * len(gens)
            while any(alive):
                for i_, g_ in enumerate(gens):
                    if alive[i_]:
                        try:
                            next(g_)
                        except StopIteration:
                            alive[i_] = False

        def wload(dst, key, src):
            tk.dma("pool", lambda: Pl.dma_start(out=dst, in_=src), writes=[key])

        def dump(name, ap, keys):
            if name in dbg_out:
                tk.dma("pool", lambda: Pl.dma_start(out=dbg_out[name], in_=ap), reads=keys)

        for l in range(NL):
            sh1, g1v = modv[:, l, 0:8], modv[:, l, 16:24]
            sh2, g2v = modv[:, l, 24:32], modv[:, l, 40:48]
            off = 0
            sz, off = carve(off, [NT, 512], BF16)
            beta, off = carve(off, [NT, 8], F32)
            gl, off = carve(off, [NT, 8], F32)
            qkvT, off_qkv = carve(off, [12, T], BF16)
            hT, off_x = carve(off_qkv, [DC, T], BF16)
            tk.dma("sp", lambda l=l: nc.sync.dma_start(out=lsm[:, 0:8], in_=a_log[l]), writes=["lsm"])
            tk.dma("sp", lambda l=l: nc.sync.dma_start(out=lsm[:, 8:16], in_=dt_bias[l]), writes=["lsm"])
            tk.dma("sp", lambda l=l: nc.sync.dma_start(out=lsm[:, 16:18], in_=pscale[l]), writes=["lsm"])
            tk.dma("sp", lambda l=l: nc.sync.dma_start(out=convw[:], in_=conv_w[l]), writes=["convw"])
            tk.op("act", lambda: Sc.activation(out=lsm[:, 0:8], in_=lsm[:, 0:8], func=AF.Exp), reads=["lsm"], writes=["lsm"])
            ea, dtb, psc = lsm[:, 0:8], lsm[:, 8:16], lsm[:, 16:18]

            regs = [[off, off_qkv], [off_x, NAR * 2]]

            def c2(shape, dt):
                nb = int(np.prod(shape)) * (4 if dt == F32 else 2)
                for r in regs:
                    if r[0] + nb <= r[1]:
                        v, r[0] = carve(r[0], shape, dt)
                        return v
                raise RuntimeError("arena full")
            wz = c2([8, 512], BF16)
            wvp = c2([8, 512], BF16)
            wu = c2([8, 256], BF16)
            wba = c2([8, 16], BF16)
            woB = c2([4, D], BF16)
            dngb = c2([1, 512], F32)
            sngb = c2([1, 256], F32)
            sqb = c2([2, 512], BF16)
            rs = c2([1, 512], F32)
            rinv = c2([1, 512], F32)
            tmp = c2([2, 512], F32)
            guT = c2([2 * 2, 512], BF16)
            ybyc = c2([2 * 4, 512], BF16)
            szt = c2([2, 512], F32)
            vg = c2([2 * 4, 256], F32)
            vn = c2([2 * 4, 256], BF16)
            vsq = c2([2, 256], F32)
            ssv = c2([2 * 4, 8], F32)
            ptok = c2([2 * 4, 256], BF16)
            dmT = c2([2 * 2, 256], BF16)
            poolm = c2([16, 128], BF16).rearrange("p (x w) n -> p x w n", x=4)
            wsT_sb = c2([4, 128], BF16)
            wp_sb = c2([2, 64], BF16)
            bsp_sb = c2([1, 512], F32)
            bal = c2([NT, 16], F32)
            wload(poolm, "poolm", poolm_in)
            wload(wsT_sb, "wsT", wsT[l])
            wload(wp_sb, "wp", wpool[l])
            tk.dma("sp", lambda l=l: nc.sync.dma_start(out=bsp_sb[0:1, 0, :], in_=bsp[l]), writes=["bsp"])
            w_in_l = w_in[l].rearrange("(kc p) n -> p kc n", p=128)
            wload(wz[:], "wz", w_in_l[:, :, C_Z:C_Z + 512])
            wload(wvp[:], "wvp", w_in_l[:, :, C_VB:C_VB + 512])
            wload(wu[:], "wu", w_in_l[:, :, C_U:C_U + 256])
            wload(wba[:], "wba", w_in_l[:, :, C_B:C_B + 16])
            wload(woB[:], "woB", w_out[l, 512:1024, :].rearrange("(kc p) n -> p kc n", p=128))
            tk.dma("sp", lambda l=l: nc.sync.dma_start(out=dngb[:, 0, :], in_=dng[l]), writes=["dngb"])
            tk.dma("sp", lambda l=l: nc.sync.dma_start(out=sngb[:, 0, :], in_=sng[l]), writes=["sngb"])
            def a1_norm(g):
                cols = slice(g * 512, (g + 1) * 512)
                return norm_mod_gen(cols, gm1[:, l, :], sh1, hT, cols, ("hT", g), sqb, rs, rinv, tmp, [("xT", g)])

            def a1_group(g):
                cols = slice(g * 512, (g + 1) * 512)
                tcs = [slice((g * 4 + tt) * 128, (g * 4 + tt + 1) * 128) for tt in range(4)]
                S_ = g % 2
                guT_ = guT[:, S_ * 2:S_ * 2 + 2, :]
                ybyc_ = ybyc[:, S_ * 4:S_ * 4 + 4, :]
                vg_ = vg[:, S_ * 4:S_ * 4 + 4, :]
                vn_ = vn[:, S_ * 4:S_ * 4 + 4, :]
                ssv_ = ssv[:, S_ * 4:S_ * 4 + 4, :]
                ptok_ = ptok[:, S_ * 4:S_ * 4 + 4, :]
                for uc in range(2):
                    b = psa.get()
                    tk.op("pe", lambda uc=uc, b=b: [Pe.matmul(PS[:, b, :], lhsT=wu[:, kc, uc * 128:(uc + 1) * 128], rhs=hT[:, kc, cols], start=(kc == 0), stop=(kc == 7)) for kc in range(8)],
                          reads=["wu", ("hT", g)], writes=[pk(b)])
                    tk.op("act", lambda uc=uc, b=b: Sc.activation(out=guT_[:, uc, :], in_=PS[:, b, :], func=AF.Gelu_apprx_tanh), reads=[pk(b)], writes=[("guT", S_)])
                    psa.put(b)
                    yield
                for tt in range(4):
                    b = psa.get()
                    tk.op("pe", lambda b=b, tt=tt: [Pe.matmul(PS[:, b, :], lhsT=hT[:, kc, tcs[tt]], rhs=wvp[:, kc, :], start=(kc == 0), stop=(kc == 7)) for kc in range(8)],
                          reads=["wvp", ("hT", g)], writes=[pk(b)])
                    tk.op("act", lambda b=b, tt=tt: Sc.activation(out=vg_[:, tt, :], in_=PS[:, b, 0:256], func=AF.Gelu_apprx_tanh), reads=[pk(b)], writes=[("vg", S_, tt)])
                    tk.op("dve", lambda b=b, tt=tt: V.tensor_copy(out=ptok_[:, tt, :], in_=PS[:, b, 256:512]), reads=[pk(b)], writes=[("ptok", S_, tt)])
                    psa.put(b)
                    yield
                    i = tt % 2
                    tk.op("pool", lambda tt=tt, i=i: Pl.tensor_tensor(out=vsq[:, i, :], in0=vg_[:, tt, :], in1=vg_[:, tt, :], op=ALU.mult), reads=[("vg", S_, tt)], writes=[("vsq", i)])
                    tk.op("dve", lambda tt=tt, i=i: V.tensor_reduce(out=ssv_[:, tt, 0:4], in_=vsq[:, i, :].rearrange("p (g c) -> p g c", g=4), axis=mybir.AxisListType.X, op=ALU.add),
                          reads=[("vsq", i)], writes=[("ssv", S_, tt)])
                    yield
                for tt in range(4):
                    t = g * 4 + tt
                    i = tt % 2
                    b = psa.get()
                    tk.op("pe", lambda b=b, tt=tt: [Pe.matmul(PS[:, b, :], lhsT=hT[:, kc, tcs[tt]], rhs=wz[:, kc, :], start=(kc == 0), stop=(kc == 7)) for kc in range(8)],
                          reads=["wz", ("hT", g)], writes=[pk(b)])
                    tk.op("act", lambda b=b, i=i: Sc.activation(out=szt[:, i, :], in_=PS[:, b, :], func=AF.Silu), reads=[pk(b)], writes=[("szt", i)])
                    psa.put(b)
                    tk.op("pool", lambda t=t, i=i: Pl.tensor_tensor(out=sz[:, t, :], in0=szt[:, i, :], in1=dngb[:, 0, :], op=ALU.mult), reads=[("szt", i), "dngb"], writes=[("sz", t)])
                    yield
                for tt in range(4):
                    t = g * 4 + tt
                    tk.op("act", lambda tt=tt: Sc.activation(out=ssv_[:, tt, 0:4], in_=ssv_[:, tt, 0:4], func=AF.Sqrt, bias=epst[:, 0:1], scale=1.0 / 64), reads=[("ssv", S_, tt), "epst"], writes=[("ssv", S_, tt)])
                    tk.op("dve", lambda tt=tt: V.reciprocal(out=ssv_[:, tt, 4:8], in_=ssv_[:, tt, 0:4]), reads=[("ssv", S_, tt)], writes=[("ssv", S_, tt)])
                    for gq in range(4):
                        tk.op("dve", lambda gq=gq, tt=tt: V.scalar_tensor_tensor(out=vn_[:, tt, gq * 64:(gq + 1) * 64], in0=vg_[:, tt, gq * 64:(gq + 1) * 64], scalar=ssv_[:, tt, 4 + gq:5 + gq],
                                                                                  in1=sngb[:, 0, gq * 64:(gq + 1) * 64], op0=ALU.mult, op1=ALU.mult),
                              reads=[("vg", S_, tt), "sngb"], sreads=[("ssv", S_, tt)], writes=[("vn", S_, tt, gq)])
                    b = psa.get()
                    tk.op("pe", lambda b=b, tt=tt: [Pe.matmul(PS[:, b, 0:16], lhsT=hT[:, kc, tcs[tt]], rhs=wba[:, kc, :], start=(kc == 0), stop=(kc == 7)) for kc in range(8)],
                          reads=["wba", ("hT", g)], writes=[pk(b)])
                    tk.op("dve", lambda b=b, t=t: V.tensor_copy(out=bal[:, t, :], in_=PS[:, b, 0:16]), reads=[pk(b)], writes=[("bal", t)])
                    psa.put(b)
                    yield
                for tt in range(4):
                    tl = slice(tt * 128, (tt + 1) * 128)
                    b = psa.get()

                    def sgu(b=b, tt=tt):
                        r = []
                        for gq in range(4):
                            o = PS[(gq % 2) * 64:(gq % 2) * 64 + 64, b, (gq // 2) * 128:(gq // 2) * 128 + 128]
                            r.append(Pe.matmul(o, lhsT=vn_[:, tt, gq * 64:(gq + 1) * 64], rhs=wsT_sb[:, gq, :], start=True, stop=False))
                            r.append(Pe.matmul(o, lhsT=onesf[0:1, 0:64], rhs=bsp_sb[0:1, 0, gq * 128:(gq + 1) * 128], start=False, stop=True))
                        return r
                    tk.op("pe", sgu, reads=[("vn", S_, tt, gq) for gq in range(4)] + ["wsT", "bsp", "onesf"], writes=[pk(b)])
                    tk.op("dve", lambda b=b, tl=tl: V.tensor_tensor(out=ybyc_[:, 0:2, tl], in0=PS[:, b, 0:256].rearrange("p (a n) -> p a n", a=2), in1=guT_[:, :, tl], op=ALU.mult),
                          reads=[pk(b), ("guT", S_)], writes=[("ybyc", S_, 0)])
                    psa.put(b)
                    yield
                for a in range(4):
                    al_ = slice(a * 128, a * 128 + 128)
                    b = psa.get()

                    def pm(b=b, a=a):
                        r = []
                        for w in range(4):
                            o = PS[(w % 2) * 64:(w % 2) * 64 + 64, b, (w // 2) * 128:(w // 2) * 128 + 128]
                            r.append(Pe.matmul(o, lhsT=ptok_[:, a, w * 64:(w + 1) * 64], rhs=poolm[:, a % 2, w, :], start=True, stop=False))
                            r.append(Pe.matmul(o, lhsT=ptok_[:, a ^ 1, w * 64:(w + 1) * 64], rhs=poolm[:, 2 + a % 2, w, :], start=False, stop=True))
                        return r
                    tk.op("pe", pm, reads=[("ptok", S_, a), ("ptok", S_, a ^ 1), "poolm"], writes=[pk(b)])
                    i = S_ * 2 + a % 2
                    tk.op("act", lambda b=b, i=i: Sc.copy(out=dmT[:, i, :], in_=PS[:, b, 0:256]), reads=[pk(b)], writes=[("dmT", i)])
                    psa.put(b)
                    yield
                    b = psa.get()

                    def pm2(b=b, i=i):
                        r = []
                        for w in range(4):
                            pr = slice((w % 2) * 64, (w % 2) * 64 + 64)
                            r.append(Pe.matmul(PS[pr, b, (w // 2) * 128:(w // 2) * 128 + 128], lhsT=wp_sb[pr, w // 2, :], rhs=dmT[pr, i, (w // 2) * 128:(w // 2) * 128 + 128], start=True, stop=True))
                        return r
                    tk.op("pe", pm2, reads=[("dmT", i), "wp"], writes=[pk(b)])
                    tk.op("act", lambda b=b, al_=al_: Sc.activation(out=ybyc_[:, 2:4, al_], in_=PS[:, b, 0:256].rearrange("p (a n) -> p a n", a=2), func=AF.Identity, scale=1.0),
                          reads=[pk(b)], writes=[("ybyc", S_, 1)])
                    psa.put(b)
                    yield
                for ch in range(2):
                    tk.op("pool", lambda ch=ch: Pl.tensor_scalar(out=ybyc_[:, 2 + ch, :], in0=ybyc_[:, 2 + ch, :], scalar1=psc[:, ch:ch + 1], scalar2=None, op0=ALU.mult),
                          reads=[("ybyc", S_, 1)], sreads=["lsm"], writes=[("ybyc", S_, 1)])
                if g == 0:
                    dump("ybyc%d" % l, ybyc_[:, :, :], [("ybyc", S_, 0), ("ybyc", S_, 1)])
                yield
                for dc in range(DC):
                    b = psa.get()
                    tk.op("pe", lambda b=b, dc=dc: [Pe.matmul(PS[:, b, :], lhsT=woB[:, kc, dc * 128:(dc + 1) * 128], rhs=ybyc_[:, kc, :], start=(kc == 0), stop=(kc == 3)) for kc in range(4)],
                          reads=["woB", ("ybyc", S_, 0), ("ybyc", S_, 1)], writes=[pk(b)])
                    tk.op("dve", lambda b=b, dc=dc: V.scalar_tensor_tensor(out=xT[:, dc, cols], in0=PS[:, b, :], scalar=g1v[:, dc:dc + 1], in1=xT[:, dc, cols], op0=ALU.mult, op1=ALU.add),
                          reads=[pk(b), "modv", ("xT", g)], writes=[("xT", g)])
                    psa.put(b)
                    yield

            ndone = [False] * NG

            def normw(g):
                yield from a1_norm(g)
                ndone[g] = True

            def groupw(g):
                while not ndone[g]:
                    yield
                yield from a1_group(g)

            def gchain(*gs):
                for g_ in gs:
                    yield from g_
            interleave(gchain(*[normw(g) for g in range(NG)]),
                       gchain(*[groupw(g) for g in range(0, NG, 2)]),
                       gchain(*[groupw(g) for g in range(1, NG, 2)]))
            tk.op("act", lambda: Sc.activation(out=beta[:, :, :], in_=bal[:, :, 0:8], func=AF.Sigmoid), reads=[("bal", t) for t in range(NT)], writes=["beta"])
            tk.op("dve", lambda: V.tensor_tensor(out=gl[:, :, :], in0=bal[:, :, 8:16], in1=bcast(dtb, 0, NT), op=ALU.add), reads=[("bal", t) for t in range(NT)] + ["lsm"], writes=["gl"])
            tk.op("act", lambda: Sc.activation(out=gl[:, :, :], in_=gl[:, :, :], func=AF.Exp), reads=["gl"], writes=["gl"])
            tk.op("act", lambda: Sc.activation(out=gl[:, :, :], in_=gl[:, :, :], func=AF.Ln, bias=epst[:, 2:3], scale=1.0), reads=["gl", "epst"], writes=["gl"])
            tk.op("dve", lambda: V.scalar_tensor_tensor(out=gl[:, :, :], in0=gl[:, :, :], scalar=-1.0, in1=bcast(ea, 0, NT), op0=ALU.mult, op1=ALU.mult), reads=["gl", "lsm"], writes=["gl"])
            dump("beta%d" % l, beta, ["beta"])
            dump("gl%d" % l, gl, ["gl"])
            dump("sz%d" % l, sz[:, :, :], [("sz", t) for t in range(NT)])
            tk.barrier()

            o2 = off_x
            raw, o2 = carve(o2, [2 * NS, 258], F32)
            acc, o2 = carve(o2, [NS, 256], F32)
            wch, o2 = carve(o2, [3 * 8, 128], BF16)
            sqb, o2 = carve(o2, [2, 512], BF16)
            rs2, o2 = carve(o2, [2, 512], F32)
            for i in range(2):
                tk.op("dve", lambda i=i: V.memset(raw[:, i * NS, 0:1], 0.0), writes=[("raw", i)])
                tk.op("dve", lambda i=i: V.memset(raw[:, i * NS + NS - 1, 257:258], 0.0), writes=[("raw", i)])

            def wch_load(ci):
                j = ci % 3
                wload(wch[:, j * 8:(j + 1) * 8, :], ("wch", j), w_in_l[:, :, ci * 128:(ci + 1) * 128])
            wch_load(0)
            wch_load(1)
            for ci in range(12):
                j = ci % 3
                i = ci % 2
                if ci + 2 < 12:
                    wch_load(ci + 2)
                rw = raw[:, i * NS:(i + 1) * NS, :]
                for g in range(NG):
                    b = psa.get()
                    tk.op("pe", lambda b=b, g=g, j=j: [Pe.matmul(PS[:, b, :], lhsT=wch[:, j * 8 + kc, :], rhs=hT[:, kc, g * 512:(g + 1) * 512], start=(kc == 0), stop=(kc == 7)) for kc in range(8)],
                          reads=[("wch", j), ("hT", g)], writes=[pk(b)])
                    tk.op("act", lambda b=b, g=g, rw=rw: Sc.copy(out=rw[:, 2 * g:2 * g + 2, 1:257], in_=PS[:, b, :].rearrange("p (s n) -> p s n", s=2)), reads=[pk(b)], writes=[("raw", i)])
                    psa.put(b)
                if NS > 1:
                    tk.op("dve", lambda rw=rw: V.tensor_scalar(out=rw[:, 1:NS, 0:1], in0=rw[:, 0:NS - 1, 256:257], scalar1=chain[:, 0:1], scalar2=None, op0=ALU.mult), reads=[("raw", i), "chain"], writes=[("raw", i)])
                    tk.op("dve", lambda rw=rw: V.tensor_scalar(out=rw[:, 0:NS - 1, 257:258], in0=rw[:, 1:NS, 1:2], scalar1=chain[:, 0:1], scalar2=None, op0=ALU.mult), reads=[("raw", i), "chain"], writes=[("raw", i)])
                for tap in (1, 0, 2):
                    for g in range(NG):
                        ag = acc[:, 2 * g:2 * g + 2, :]
                        rg = rw[:, 2 * g:2 * g + 2, :]
                        ka = ("acc", g)
                        if tap == 1:
                            tk.op("dve", lambda rg=rg, ag=ag, ci=ci: V.tensor_scalar(out=ag, in0=rg[:, :, 1:257], scalar1=convw[:, ci, 1:2], scalar2=None, op0=ALU.mult), reads=[("raw", i), "convw"], writes=[ka])
                        else:
                            tk.op("dve", lambda rg=rg, ag=ag, ci=ci, tap=tap: V.scalar_tensor_tensor(out=ag, in0=rg[:, :, tap:tap + 256], scalar=convw[:, ci, tap:tap + 1], in1=ag, op0=ALU.mult, op1=ALU.add),
                                  reads=[("raw", i), "convw", ka], writes=[ka])
                for g in range(NG):
                    cs = slice(g * 512, (g + 1) * 512)
                    ag = acc[:, 2 * g:2 * g + 2, :]
                    tk.op("act", lambda ag=ag, cs=cs, ci=ci: Sc.activation(out=qkvT[:, ci, cs].rearrange("p (s n) -> p s n", s=2), in_=ag, func=AF.Silu), reads=[("acc", g)], writes=[("qkvT", ci, g)])
            for ci in range(8):
                for g in range(NG):
                    cs = slice(g * 512, (g + 1) * 512)
                    ii = (ci * NG + g) % 2
                    kq = ("qkvT", ci, g)
                    b = psa.get()
                    tk.op("pool", lambda ii=ii, cs=cs, ci=ci: Pl.tensor_tensor(out=sqb[:, ii, :], in0=qkvT[:, ci, cs], in1=qkvT[:, ci, cs], op=ALU.mult), reads=[kq], writes=[("sqb", ii)])
                    tk.op("pe", lambda ii=ii, b=b: Pe.matmul(PS[:, b, :], lhsT=onesb[:], rhs=sqb[:, ii, :], start=True, stop=True), reads=[("sqb", ii), "onesb"], writes=[pk(b)])
                    sc_, eb_ = (128.0, epst[:, 1:2]) if ci < 4 else (1.0, epst[:, 0:1])
                    tk.op("act", lambda b=b, ii=ii, sc_=sc_, eb_=eb_: Sc.activation(out=rs2[:, ii, :], in_=PS[:, b, :], func=AF.Ln, bias=eb_, scale=sc_), reads=[pk(b), "epst"], writes=[("rs2", ii)])
                    psa.put(b)
                    tk.op("act", lambda ii=ii: Sc.activation(out=rs2[:, ii, :], in_=rs2[:, ii, :], func=AF.Exp, scale=-0.5), reads=[("rs2", ii)], writes=[("rs2", ii)])
                    tk.op("dve", lambda cs=cs, ci=ci, ii=ii: V.tensor_tensor(out=qkvT[:, ci, cs], in0=qkvT[:, ci, cs], in1=rs2[:, ii, :], op=ALU.mult), reads=[kq, ("rs2", ii)], writes=[kq])
            dump("qkvT%d" % l, qkvT[:, :, :], [("qkvT", ci, g) for ci in range(12) for g in range(NG)])
            tk.barrier()

            ob = off_qkv
            ostore, ob = carve(ob, [NT, 512], BF16)
            Sf, ob = carve(ob, [2, 512], F32)
            Sb, ob = carve(ob, [2, 512], BF16)
            yaT, ob = carve(ob, [2 * 4, 512], BF16)
            BUF = []
            for d in range(2):
                Bd = {}
                Bd["Dm"], ob = carve(ob, [4, 128], F32)
                Bd["Em"], ob = carve(ob, [4, 128], BF16)
                Bd["Rm"], ob = carve(ob, [4, 256], BF16)
                Bd["LT"], ob = carve(ob, [4, 128], BF16)
                Bd["Lm"], ob = carve(ob, [4, 128], BF16)
                Bd["Tm"], ob = carve(ob, [2 * 4, 128], BF16)
                Bd["TTm"], ob = carve(ob, [2 * 4, 128], BF16)
                Bd["egr"], ob = carve(ob, [4, 128], BF16)
                Bd["sm"], ob = carve(ob, [2, 32], F32)
                Bd["AT"], ob = carve(ob, [4, 128], BF16)
                Bd["kd"], ob = carve(ob, [4, 128], BF16)
                Bd["UW"], ob = carve(ob, [4, 256], BF16)
                Bd["wq"], ob = carve(ob, [2 * 4, 128], BF16)
                Bd["wT"] = Bd["wq"][:, 0:4, :]
                Bd["qdT"] = Bd["wq"][:, 4:8, :]
                Bd["vnew"], ob = carve(ob, [4, 128], BF16)
                Bd["gss"], ob = carve(ob, [1, 8], F32)
                Bd["yat"] = Bd["vnew"]
                BUF.append(Bd)
            woA_res = (ob + 4 * D * 2 <= NAR * 2)
            if woA_res:
                woA, ob = carve(ob, [4, D], BF16)
                wload(woA[:], "woA", w_out[l, 0:512, :].rearrange("(kc p) n -> p kc n", p=128))
            else:
                woS, ob = carve(ob, [2 * 4, 128], BF16)
            for d in range(2):
                tk.dma("sp", lambda d=d, l=l: nc.sync.dma_start(out=Sf[:, d, :].rearrange("p (h v) -> p h v", h=4), in_=s0in[l, d]), writes=[("Sf", d, h_) for h_ in range(4)])
                tk.op("act", lambda d=d: Sc.copy(out=Sb[:, d, :], in_=Sf[:, d, :]), reads=[("Sf", d, h_) for h_ in range(4)], writes=[("Sb", d)])
            gated = [0] * NG
            arrived = [0] * NT

            def pre(t, d):
                Bd = BUF[d]
                K = lambda n: (n, d)
                Dm, Em, Rm, LT, Lm, Tm, TTm, egr = Bd["Dm"], Bd["Em"], Bd["Rm"], Bd["LT"], Bd["Lm"], Bd["Tm"], Bd["TTm"], Bd["egr"]
                ATm, kdm, UWm, wTm, qdT = Bd["AT"], Bd["kd"], Bd["UW"], Bd["wT"], Bd["qdT"]
                Ebm = Lm[:, :, :]
                Ym = Dm[:, :, :].rearrange("p a n -> p (a n)").bitcast(BF16)[:, 0:512].rearrange("p (a n) -> p a n", a=4)
                tcols = slice(t * 128, (t + 1) * 128)
                hd = slice(d * 4, d * 4 + 4)
                sm = Bd["sm"][:, t % 2, :]
                KS = ("sm", d, t % 2)
                tri = cm[:, TRI_F + d, :]
                bkv = psa.get()
                ktok = PS[:, bkv, 0:256].bitcast(BF16).rearrange("p (h n) -> p h n", h=4)
                vtok = PS[:, bkv, 256:512].bitcast(BF16).rearrange("p (h n) -> p h n", h=4)
                tk.op("pe", lambda: [Pe.transpose(out=ktok[:, h, :], in_=qkvT[:, 4 + h, tcols], identity=identb) for h in range(4)]
                      + [Pe.transpose(out=vtok[:, h, :], in_=qkvT[:, 8 + h, tcols], identity=identb) for h in range(4)],
                      reads=[("qkvT", c) for c in range(4, 12)] + ["cmb"], writes=[pk(bkv)])
                bs_ = psa.get()
                tk.op("pe", lambda: [Pe.matmul(PS[:, bs_, 0:4], lhsT=tri, rhs=gl[:, t, hd], start=True, stop=True),
                                     Pe.matmul(PS[:, bs_, 4:8], lhsT=cm[:, BLK, :], rhs=gl[:, t, hd], start=True, stop=True)],
                      reads=["cm", "gl"], writes=[pk(bs_)])
                tk.op("dve", lambda: V.tensor_copy(out=sm[:, 0:8], in_=PS[:, bs_, 0:8]), reads=[pk(bs_)], writes=[KS])
                psa.put(bs_)
                tk.op("dve", lambda: V.tensor_copy(out=Dm[:, :, :], in_=bcast(gl[:, t, hd], 1, 128)), reads=["gl"], writes=[K("Dm")])
                yield
                bg = psa.get()
                gcrow = PS[:, bg, :].rearrange("p (h n) -> p h n", h=4)
                tk.op("pe", lambda: [Pe.matmul(gcrow[:, h, :], lhsT=Dm[:, h, :], rhs=tri, start=True, stop=True) for h in range(4)], reads=[K("Dm"), "cm"], writes=[pk(bg)])
                yield
                tk.op("dve", lambda: V.tensor_tensor(out=Dm[:, :, :], in0=gcrow, in1=bcast(sm[:, 0:4], 1, 128), op=ALU.subtract), reads=[pk(bg), KS], writes=[K("Dm")])
                tk.op("act", lambda: Sc.activation(out=egr[:, :, :], in_=gcrow, func=AF.Exp), reads=[pk(bg), K("Dm")], writes=[K("egr")])
                lastcol = (63, 127) if d == 0 else (0, 64)
                for c in range(2):
                    tk.op("act", lambda c=c: Sc.activation(out=sm[:, 16 + 4 * c:20 + 4 * c], in_=gcrow[:, :, lastcol[c]], func=AF.Exp), reads=[pk(bg)], writes=[KS])
                psa.put(bg)
                yield
                tk.op("pool", lambda: Pl.tensor_tensor(out=Dm[:, :, :], in0=Dm[:, :, :], in1=bcast(cm[:, NEG_F + d, :], 0, 4), op=ALU.add), reads=[K("Dm"), "cm"], writes=[K("Dm")])
                tk.op("dve", lambda: V.tensor_tensor(out=sm[:, 12:16], in0=sm[:, 4:8], in1=sm[:, 0:4], op=ALU.subtract), reads=[KS], writes=[KS])
                yield
                tk.op("act", lambda: Sc.activation(out=Em[:, :, :], in_=Dm[:, :, :], func=AF.Exp), reads=[K("Dm")], writes=[K("Em")])
                tk.op("act", lambda: Sc.activation(out=sm[:, 8:12], in_=sm[:, 0:4], func=AF.Exp), reads=[KS], writes=[KS])
                tk.op("act", lambda: Sc.activation(out=sm[:, 12:16], in_=sm[:, 12:16], func=AF.Exp), reads=[KS], writes=[KS])
                bkk = psa.get()
                kk = PS[:, bkk, :].rearrange("p (h n) -> p h n", h=4)
                tk.op("pe", lambda: [Pe.matmul(kk[:, h, :], lhsT=qkvT[:, 4 + h, tcols], rhs=qkvT[:, 4 + h, tcols], start=True, stop=True) for h in range(4)],
                      reads=[("qkvT", c) for c in range(4, 8)], writes=[pk(bkk)])
                tk.op("act", lambda: Sc.copy(out=Rm[:, :, 0:128], in_=vtok), reads=[pk(bkv)], writes=[K("Rm")])
                tk.op("dve", lambda: V.tensor_tensor(out=Rm[:, :, 128:256], in0=ktok, in1=bcast(sm[:, 8:12], 1, 128), op=ALU.mult), reads=[pk(bkv), KS], writes=[K("Rm")])
                psa.put(bkv)
                yield
                strict = cmb[:, STR_F + d, :]
                tk.op("pool", lambda: Pl.tensor_tensor(out=Ebm, in0=Em[:, :, :], in1=bcast(strict, 0, 4), op=ALU.mult), reads=[K("Em"), "cmb"], writes=[K("Lm")])
                yield
                tk.op("pool", lambda: Pl.tensor_tensor(out=Ebm, in0=Ebm, in1=bcast(beta[:, t, hd], 1, 128), op=ALU.mult), reads=[K("Lm"), "beta"], writes=[K("Lm")])
                yield
                tk.op("dve", lambda: V.tensor_tensor(out=LT[:, :, :], in0=kk, in1=Ebm, op=ALU.mult), reads=[pk(bkk), K("Lm")], writes=[K("LT")])
                psa.put(bkk)
                yield
                cur = 0
                tk.op("pool", lambda: Pl.tensor_tensor(out=Lm[:, :, :], in0=LT[:, :, :], in1=bcast(cmb[:, LM0, :], 0, 4), op=ALU.mult), reads=[K("LT"), "cmb"], writes=[K("Lm")])
                tk.op("pool", lambda: Pl.tensor_tensor(out=TTm[:, 0:4, :], in0=bcast(identb, 0, 4), in1=Lm[:, :, :], op=ALU.subtract), reads=[K("Lm"), "cmb"], writes=[K("TT0")])
                yield
                bt = psa.get()
                tps = PS[:, bt, 0:256].bitcast(BF16).rearrange("p (h n) -> p h n", h=4)
                tk.op("pe", lambda: [Pe.transpose(out=tps[:, h, :], in_=TTm[:, h, :], identity=identb) for h in range(4)], reads=[K("TT0"), "cmb"], writes=[pk(bt)])
                yield
                tk.op("act", lambda: Sc.copy(out=Tm[:, 0:4, :], in_=tps), reads=[pk(bt)], writes=[K("T0")])
                psa.put(bt)
                for k in range(1, 6):
                    nx = 1 - cur
                    tk.op("pool", lambda k=k: Pl.tensor_tensor(out=Lm[:, :, :], in0=LT[:, :, :], in1=bcast(cmb[:, LM0 + k, :], 0, 4), op=ALU.mult), reads=[K("LT"), "cmb"], writes=[K("Lm")])
                    yield
                    by = psa.get()
                    yps = PS[:, by, :].rearrange("p (h n) -> p h n", h=4)
                    tk.op("pe", lambda cur=cur: [Pe.matmul(yps[:, h, :], lhsT=Lm[:, h, :], rhs=Tm[:, cur * 4 + h, :], start=True, stop=True) for h in range(4)], reads=[K("Lm"), K("T%d" % cur)], writes=[pk(by)])
                    yield
                    tk.op("act", lambda: Sc.mul(out=Ym[:, :, :], in_=yps, mul=-1.0), reads=[pk(by)], writes=[K("Dm")])
                    psa.put(by)
                    yield
                    bz = psa.get()
                    zps = PS[:, bz, :].rearrange("p (h n) -> p h n", h=4)

                    def mz(cur=cur, zps=zps):
                        r = []
                        for h in range(4):
                            r.append(Pe.matmul(zps[:, h, :], lhsT=identb, rhs=TTm[:, cur * 4 + h, :], start=True, stop=False))
                            r.append(Pe.matmul(zps[:, h, :], lhsT=Ym[:, h, :], rhs=TTm[:, cur * 4 + h, :], start=False, stop=True))
                        return r
                    tk.op("pe", mz, reads=[K("Dm"), K("TT%d" % cur), "cmb"], writes=[pk(bz)])
                    bz2 = None
                    if k < 5:
                        bz2 = psa.get()
                        zps2 = PS[:, bz2, :].rearrange("p (h n) -> p h n", h=4)

                        def mz2(cur=cur, zps2=zps2):
                            r = []
                            for h in range(4):
                                r.append(Pe.matmul(zps2[:, h, :], lhsT=identb, rhs=Tm[:, cur * 4 + h, :], start=True, stop=False))
                                r.append(Pe.matmul(zps2[:, h, :], lhsT=TTm[:, cur * 4 + h, :], rhs=Ym[:, h, :], start=False, stop=True))
                            return r
                        tk.op("pe", mz2, reads=[K("Dm"), K("TT%d" % cur), K("T%d" % cur), "cmb"], writes=[pk(bz2)])
                    yield
                    tk.op("act", lambda nx=nx: Sc.copy(out=TTm[:, nx * 4:nx * 4 + 4, :], in_=zps), reads=[pk(bz)], writes=[K("TT%d" % nx)])
                    psa.put(bz)
                    if k < 5:
                        tk.op("dve", lambda nx=nx: V.tensor_copy(out=Tm[:, nx * 4:nx * 4 + 4, :], in_=zps2), reads=[pk(bz2)], writes=[K("T%d" % nx)])
                        psa.put(bz2)
                    cur = nx
                    yield
                yield "BAR"
                bqk = psa.get()
                qk = PS[:, bqk, :].rearrange("p (h n) -> p h n", h=4)
                tk.op("pe", lambda: [Pe.matmul(qk[:, h, :], lhsT=qkvT[:, 4 + h, tcols], rhs=qkvT[:, h, tcols], start=True, stop=True) for h in range(4)],
                      reads=[("qkvT", c) for c in range(0, 8)], writes=[pk(bqk)])
                bk2 = psa.get()
                ktok2 = PS[:, bk2, 0:256].bitcast(BF16).rearrange("p (h n) -> p h n", h=4)
                tk.op("pe", lambda: [Pe.transpose(out=ktok2[:, h, :], in_=qkvT[:, 4 + h, tcols], identity=identb) for h in range(4)],
                      reads=[("qkvT", c) for c in range(4, 8)] + ["cmb"], writes=[pk(bk2)])
                yield
                tk.op("dve", lambda: V.tensor_tensor(out=ATm[:, :, :], in0=qk, in1=Em[:, :, :], op=ALU.mult), reads=[pk(bqk), K("Em")], writes=[K("AT")])
                psa.put(bqk)
                tk.op("dve", lambda: V.tensor_tensor(out=kdm[:, :, :], in0=ktok2, in1=bcast(sm[:, 12:16], 1, 128), op=ALU.mult), reads=[pk(bk2), KS], writes=[K("kd")])
                psa.put(bk2)
                yield
                for half in range(2):
                    bu = psa.get()
                    uw = PS[:, bu, :].rearrange("p (h n) -> p h n", h=2)
                    tk.op("pe", lambda cur=cur, half=half, uw=uw: [Pe.matmul(uw[:, hh, :], lhsT=TTm[:, cur * 4 + half * 2 + hh, :], rhs=Rm[:, half * 2 + hh, :], start=True, stop=True) for hh in range(2)],
                          reads=[K("Rm"), K("TT%d" % cur)], writes=[pk(bu)])
                    yield
                    tk.op("dve", lambda half=half, uw=uw: V.tensor_tensor(out=UWm[:, half * 2:half * 2 + 2, :], in0=uw, in1=bcast(beta[:, t, d * 4 + half * 2:d * 4 + half * 2 + 2], 1, 256), op=ALU.mult),
                          reads=[pk(bu), "beta"], writes=[K("UW")])
                    psa.put(bu)
                yield
                bw = psa.get()
                wps = PS[:, bw, 0:256].bitcast(BF16).rearrange("p (h n) -> p h n", h=4)
                tk.op("pe", lambda: [Pe.transpose(out=wps[:, h, :], in_=UWm[:, h, 128:256], identity=identb) for h in range(4)], reads=[K("UW"), "cmb"], writes=[pk(bw)])
                tk.op("pool", lambda: Pl.tensor_tensor(out=qdT[:, :, :], in0=qkvT[:, 0:4, tcols], in1=egr[:, :, :], op=ALU.mult), reads=[("qkvT", c) for c in range(4)] + [K("egr")], writes=[K("qdT")])
                yield
                tk.op("act", lambda: Sc.copy(out=wTm[:, :, :], in_=wps), reads=[pk(bw)], writes=[K("wT")])
                psa.put(bw)
                yield

            def scan(t, d):
                Bd = BUF[d]
                K = lambda n: (n, d)
                ATm, kdm, UWm, wTm, qdT, vnew = Bd["AT"], Bd["kd"], Bd["UW"], Bd["wT"], Bd["qdT"], Bd["vnew"]
                sm = Bd["sm"][:, t % 2, :]
                KS = ("sm", d, t % 2)
                bo = psa.get()
                ops = PS[:, bo, :].rearrange("p (h n) -> p h n", h=4)
                order = (0, 1) if d == 0 else (1, 0)
                for c in order:
                    pr = slice(c * 64, c * 64 + 64)
                    bw_ = psa.get()
                    wsps = PS[pr, bw_, :].rearrange("p (h n) -> p h n", h=4)
                    tk.op("pe", lambda: [Pe.matmul(wsps[:, h, :], lhsT=wTm[:, h, pr], rhs=Sb[:, d, h * 128:(h + 1) * 128], start=True, stop=True) for h in range(4)],
                          reads=[K("wT"), ("Sb", d)], writes=[pk(bw_)])
                    yield
                    tk.op("dve", lambda: V.tensor_tensor(out=vnew[pr, 0:4, :], in0=UWm[pr, 0:4, 0:128], in1=wsps, op=ALU.subtract), reads=[pk(bw_), K("UW")], writes=[K("vnew")])
                    psa.put(bw_)
                    yield

                    def omm():
                        r = []
                        for h in range(4):
                            r.append(Pe.matmul(ops[pr, h, :], lhsT=qdT[:, h, pr], rhs=Sb[:, d, h * 128:(h + 1) * 128], start=True, stop=False))
                            r.append(Pe.matmul(ops[pr, h, :], lhsT=ATm[pr, h, pr], rhs=vnew[pr, h, :], start=False, stop=True))
                        return r
                    bd = psa.get()
                    dps = PS[:, bd, :].rearrange("p (h n) -> p h n", h=4)
                    tk.op("pe", lambda: [Pe.matmul(dps[:, h, :], lhsT=kdm[pr, h, :], rhs=vnew[pr, h, :], start=True, stop=True) for h in range(4)],
                          reads=[K("kd"), K("vnew")], writes=[pk(bd)])
                    tk.op("pe", omm, reads=[K("qdT"), ("Sb", d), K("AT"), K("vnew")], writes=[pk(bo)])
                    yield
                    for h in range(4):
                        tk.op("dve", lambda h=h: V.scalar_tensor_tensor(out=Sf[:, d, h * 128:(h + 1) * 128], in0=Sf[:, d, h * 128:(h + 1) * 128], scalar=sm[:, 16 + 4 * c + h:17 + 4 * c + h],
                                                                         in1=dps[:, h, :], op0=ALU.mult, op1=ALU.add), reads=[pk(bd), ("Sf", d, h)], sreads=[KS], writes=[("Sf", d, h)])
                    psa.put(bd)
                    yield
                    tk.op("act", lambda: Sc.copy(out=Sb[:, d, :], in_=Sf[:, d, :]), reads=[("Sf", d, h_) for h_ in range(4)], writes=[("Sb", d)])
                    yield
                arrived[t] += 1
                if arrived[t] == 1:
                    tk.op("act", lambda: Sc.copy(out=ostore[:, t, :].rearrange("p (h n) -> p h n", h=4), in_=ops), reads=[pk(bo)], writes=[("ost", t)])
                    psa.put(bo)
                    yield
                else:
                    ot = Bd["UW"][:, :, :].rearrange("p a n -> p (a n)").bitcast(F32).rearrange("p (a n) -> p a n", a=4)
                    osq = Bd["wq"][:, :, :].rearrange("p a n -> p (a n)").bitcast(F32).rearrange("p (a n) -> p a n", a=4)
                    gss, yat = Bd["gss"], Bd["yat"]
                    tk.op("dve", lambda: V.tensor_tensor(out=ot[:, :, :], in0=ops, in1=ostore[:, t, :].rearrange("p (h n) -> p h n", h=4), op=ALU.add), reads=[pk(bo), ("ost", t)], writes=[K("UW")])
                    psa.put(bo)
                    yield
                    tk.op("pool", lambda: Pl.tensor_tensor(out=osq, in0=ot[:, :, :], in1=ot[:, :, :], op=ALU.mult), reads=[K("UW")], writes=[K("wT"), K("qdT")])
                    yield
                    tk.op("dve", lambda: V.tensor_reduce(out=gss[:, 0, 0:4], in_=osq, axis=mybir.AxisListType.X, op=ALU.add), reads=[K("wT"), K("qdT")], writes=[K("gss")])
                    tk.op("act", lambda: Sc.activation(out=gss[:, 0, 0:4], in_=gss[:, 0, 0:4], func=AF.Ln, bias=epst[:, 0:1], scale=1.0 / 128), reads=[K("gss"), "epst"], writes=[K("gss")])
                    tk.op("act", lambda: Sc.activation(out=gss[:, 0, 4:8], in_=gss[:, 0, 0:4], func=AF.Exp, scale=-0.5), reads=[K("gss")], writes=[K("gss")])
                    yield
                    for h in range(4):
                        tk.op("dve", lambda h=h: V.scalar_tensor_tensor(out=yat[:, h, :], in0=ot[:, h, :], scalar=gss[:, 0, 4 + h:5 + h], in1=sz[:, t, h * 128:(h + 1) * 128], op0=ALU.mult, op1=ALU.mult),
                              reads=[K("UW"), ("sz", t)], sreads=[K("gss")], writes=[K("vnew")])
                    yield
                    g = t // 4
                    gi = g % 2
                    by = psa.get()
                    yps = PS[:, by, 0:256].bitcast(BF16).rearrange("p (h n) -> p h n", h=4)
                    tk.op("pe", lambda: [Pe.transpose(out=yps[:, h, :], in_=yat[:, h, :], identity=identb) for h in range(4)], reads=[K("vnew"), "cmb"], writes=[pk(by)])
                    yield
                    tk.op("act", lambda: Sc.copy(out=yaT[:, gi * 4:gi * 4 + 4, (t % 4) * 128:(t % 4) * 128 + 128], in_=yps), reads=[pk(by)], writes=[("yaT", gi)])
                    psa.put(by)
                    gated[g] += 1
                    do_out = (gated[g] == 4)
                    yield
                    if do_out:
                        cols = slice(g * 512, (g + 1) * 512)
                        if g == 0:
                            dump("yaT%d" % l, yaT[:, gi * 4:gi * 4 + 4, :], [("yaT", gi)])
                        for dc in range(DC):
                            if woA_res:
                                wsl = lambda kc, dc=dc: woA[:, kc, dc * 128:(dc + 1) * 128]
                                wkeys = ["woA"]
                            else:
                                jj = dc % 2
                                wload(woS[:, jj * 4:jj * 4 + 4, :], ("woS", jj), w_out[l, 0:512, dc * 128:(dc + 1) * 128].rearrange("(kc p) n -> p kc n", p=128))
                                wsl = lambda kc, jj=jj: woS[:, jj * 4 + kc, :]
                                wkeys = [("woS", jj)]
                            b = psa.get()
                            tk.op("pe", lambda b=b, wsl=wsl: [Pe.matmul(PS[:, b, :], lhsT=wsl(kc), rhs=yaT[:, gi * 4 + kc, :], start=(kc == 0), stop=(kc == 3)) for kc in range(4)],
                                  reads=wkeys + [("yaT", gi)], writes=[pk(b)])
                            yield
                            tk.op("dve", lambda b=b, dc=dc: V.scalar_tensor_tensor(out=xT[:, dc, cols], in0=PS[:, b, :], scalar=g1v[:, dc:dc + 1], in1=xT[:, dc, cols], op0=ALU.mult, op1=ALU.add),
                                  reads=[pk(b), "modv", ("xT", g)], writes=[("xT", g)])
                            psa.put(b)
                            yield

            def slot_end(t, d):
                slot = t // 2
                tk.dma("sp", lambda: nc.sync.dma_start(out=stout[slot, l, d].rearrange("h k v -> k h v"), in_=Sf[:, d, :].rearrange("p (h v) -> p h v", h=4)), reads=[("Sf", d, h_) for h_ in range(4)])
                tk.op("dve", lambda: V.tensor_scalar(out=Sf[:, d, :], in0=Sf[:, d, :], scalar1=chain[:, 0:1], scalar2=None, op0=ALU.mult), reads=[("Sf", d, h_) for h_ in range(4)] + ["chain"], writes=[("Sf", d, h_) for h_ in range(4)])
                tk.op("act", lambda: Sc.copy(out=Sb[:, d, :], in_=Sf[:, d, :]), reads=[("Sf", d, h_) for h_ in range(4)], writes=[("Sb", d)])

            def scan_full(t, d):
                yield from scan(t, d)
                if (d == 0 and t % 2 == 1) or (d == 1 and t % 2 == 0):
                    slot_end(t, d)
                    yield

            def stream(d):
                tiles = list(range(NT)) if d == 0 else list(range(NT - 1, -1, -1))
                for _ in pre(tiles[0], d):
                    yield
                for idx, t in enumerate(tiles):
                    gs = scan_full(t, d)
                    gp = pre(tiles[idx + 1], d) if (idx + 1 < NT and OVERLAP) else None
                    s_alive, p_alive, p_bar = True, gp is not None, False
                    while s_alive or p_alive:
                        if s_alive:
                            try:
                                next(gs)
                                yield
                            except StopIteration:
                                s_alive = False
                        if p_alive and not (p_bar and s_alive):
                            try:
                                tok = next(gp)
                                if tok == "BAR" and s_alive:
                                    p_bar = True
                                yield
                            except StopIteration:
                                p_alive = False
                    if gp is None and idx + 1 < NT:
                        for _ in pre(tiles[idx + 1], d):
                            yield

            gens = [stream(0), stream(1)]
            alive = [True, True]
            if SKIPB:
                gens = []
                alive = [False, False]
            if not INTERLEAVE:
                for gg_ in gens:
                    for n_, _ in enumerate(gg_):
                        if n_ >= BLIMIT:
                            break
                alive = [False, False]
            while any(alive):
                for gi_ in range(2):
                    if alive[gi_]:
                        try:
                            next(gens[gi_])
                        except StopIteration:
                            alive[gi_] = False
            dump("xmid%d" % l, xT[:, :, :], [("xT", g) for g in range(NG)])
            tk.barrier()

            od = 0
            h2T, od = carve(od, [DC, HT], BF16)
            actT, od = carve(od, [FC, HT], BF16)
            wdn, od = carve(od, [FC, D], BF16)
            wgu, od = carve(od, [3 * 2 * 8, 128], BF16)
            sqb, od = carve(od, [2, 512], BF16)
            rs, od = carve(od, [1, 512], F32)
            rinv, od = carve(od, [1, 512], F32)
            tmp, od = carve(od, [2, 512], F32)
            sg, od = carve(od, [2, 512], BF16)
            w_gu_l = w_gu[l].rearrange("(kc p) n -> p kc n", p=128)
            for fq in range(0, FC, 2):
                n = min(2, FC - fq)
                wload(wdn[:, fq:fq + n, :], ("wdn", fq // 2), w_down[l, fq * 128:(fq + n) * 128, :].rearrange("(f p) n -> p f n", p=128))

            def wgu_load(f):
                j = f % 3
                wload(wgu[:, j * 16:j * 16 + 8, :], ("wgu", j, 0), w_gu_l[:, :, f * 128:(f + 1) * 128])
                wload(wgu[:, j * 16 + 8:j * 16 + 16, :], ("wgu", j, 1), w_gu_l[:, :, FF + f * 128:FF + (f + 1) * 128])
            def ffn_norm(hf):
                for gg in range(NGH):
                    g = hf * NGH + gg
                    yield from norm_mod_gen(slice(g * 512, (g + 1) * 512), gm2[:, l, :], sh2, h2T, slice(gg * 512, (gg + 1) * 512), ("h2T", gg), sqb, rs, rinv, tmp, [("xT", g)])

            def ffn_down(hf):
                for gg in range(NGH):
                    g = hf * NGH + gg
                    hc = slice(gg * 512, (gg + 1) * 512)
                    cols = slice(g * 512, (g + 1) * 512)
                    for dc in range(DC):
                        b = psa.get()
                        tk.op("pe", lambda b=b, dc=dc, hc=hc: [Pe.matmul(PS[:, b, :], lhsT=wdn[:, f, dc * 128:(dc + 1) * 128], rhs=actT[:, f, hc], start=(f == 0), stop=(f == FC - 1)) for f in range(FC)],
                              reads=[("wdn", q) for q in range(FC // 2)] + [("actT", gg)], writes=[pk(b)])
                        tk.op("dve", lambda b=b, dc=dc, cols=cols: V.scalar_tensor_tensor(out=xT[:, dc, cols], in0=PS[:, b, :], scalar=g2v[:, dc:dc + 1], in1=xT[:, dc, cols], op0=ALU.mult, op1=ALU.add),
                              reads=[pk(b), "modv", ("xT", g)], writes=[("xT", g)])
                        psa.put(b)
                        yield

            for hf in range(NHALF):
                if hf == 0:
                    for _ in ffn_norm(0):
                        pass
                wgu_load(0)
                wgu_load(1)
                for f in range(FC):
                    j = f % 3
                    if f + 2 < FC:
                        wgu_load(f + 2)
                    for gg in range(NGH):
                        hc = slice(gg * 512, (gg + 1) * 512)
                        i = (f * NGH + gg) % 2
                        b1 = psa.get()
                        b2 = psa.get()
                        tk.op("pe", lambda b1=b1, hc=hc, j=j: [Pe.matmul(PS[:, b1, :], lhsT=wgu[:, j * 16 + kc, :], rhs=h2T[:, kc, hc], start=(kc == 0), stop=(kc == 7)) for kc in range(8)],
                              reads=[("wgu", j, 0), ("h2T", gg)], writes=[pk(b1)])
                        tk.op("pe", lambda b2=b2, hc=hc, j=j: [Pe.matmul(PS[:, b2, :], lhsT=wgu[:, j * 16 + 8 + kc, :], rhs=h2T[:, kc, hc], start=(kc == 0), stop=(kc == 7)) for kc in range(8)],
                              reads=[("wgu", j, 1), ("h2T", gg)], writes=[pk(b2)])
                        tk.op("act", lambda b1=b1, i=i: Sc.activation(out=sg[:, i, :], in_=PS[:, b1, :], func=AF.Silu), reads=[pk(b1)], writes=[("sg", i)])
                        tk.op("dve", lambda b2=b2, i=i, f=f, hc=hc: V.tensor_tensor(out=actT[:, f, hc], in0=PS[:, b2, :], in1=sg[:, i, :], op=ALU.mult), reads=[pk(b2), ("sg", i)], writes=[("actT", gg)])
                        psa.put(b1)
                        psa.put(b2)
                interleave(ffn_down(hf), ffn_norm(hf + 1) if hf + 1 < NHALF else None)
            dump("xend%d" % l, xT[:, :, :], [("xT", g) for g in range(NG)])
            tk.barrier()

        of = 0
        sqb, of = carve(of, [2, 512], BF16)
        rs, of = carve(of, [1, 512], F32)
        rinv, of = carve(of, [1, 512], F32)
        yfin, of = carve(of, [DC, 512], F32)
        ytok, of = carve(of, [2, D], F32)
        for g in range(NG):
            cols = slice(g * 512, (g + 1) * 512)
            rms_rinv(cols, 1.0 / D, sqb, rs, rinv, [("xT", g)])
            for c in range(DC):
                tk.op("dve", lambda c=c: V.scalar_tensor_tensor(out=yfin[:, c, :], in0=xT[:, c, cols], scalar=nfg_sb[:, c:c + 1], in1=rinv[:, 0, :], op0=ALU.mult, op1=ALU.mult),
                      reads=[("xT", g), "rinv", "nfg"], writes=["yfin"])
            for tt in range(4):
                t = g * 4 + tt
                i = t % 2
                for hh in range(2):
                    b = psa.get()
                    tk.op("pe", lambda b=b, hh=hh, tt=tt: [Pe.transpose(out=PS[:, b, j * 128:(j + 1) * 128], in_=yfin[:, hh * 4 + j, tt * 128:(tt + 1) * 128], identity=identf) for j in range(4)],
                          reads=["yfin", "cm"], writes=[pk(b)])
                    if hh == 0:
                        tk.op("act", lambda b=b, i=i: Sc.copy(out=ytok[:, i, 0:512], in_=PS[:, b, :]), reads=[pk(b)], writes=[("ytok", i)])
                    else:
                        tk.op("dve", lambda b=b, i=i: V.tensor_copy(out=ytok[:, i, 512:1024], in_=PS[:, b, :]), reads=[pk(b)], writes=[("ytok", i)])
                    psa.put(b)
                tk.dma("sp", lambda t=t, i=i: nc.sync.dma_start(out=yout[t * 128:(t + 1) * 128, :], in_=ytok[:, i, :]), reads=[("ytok", i)])
        for e in ("sp", "act", "pool", "dve", "pe"):
            tk.wait_all(e)
        build.stats = (tk.nins, tk.nwait, tk.nsem)
    return nc


def _masks():
    j = np.arange(128)[:, None]
    i = np.arange(128)[None, :]
    same = (j // 64) == (i // 64)
    f = np.zeros((6, 128, 128), np.float32)
    f[0] = (j == i)
    f[1] = same & (j <= i)
    f[2] = same & (j >= i)
    f[3] = same
    f[4] = np.where(same & (i >= j), 0.0, NEG)
    f[5] = np.where(same & (i <= j), 0.0, NEG)
    b = np.zeros((9, 128, 128), np.float32)
    b[0] = (j == i)
    b[1] = same & (i > j)
    b[2] = same & (i < j)
    for k in range(6):
        bs = 1 << k
        b[3 + k] = ((j // (2 * bs)) == (i // (2 * bs))) & ((j // bs) != (i // bs))
    return (np.ascontiguousarray(f.transpose(1, 0, 2)), np.ascontiguousarray(b.transpose(1, 0, 2)))


def _pool_mats(seg):
    wins = (2, 4, 8, 16)
    P = np.zeros((4, 4, 128, 128), np.float64)
    for wi, win in enumerate(wins):
        for par in range(2):
            for to in range(128):
                pos_slot = par * 128 + to
                s0 = (pos_slot // seg) * seg
                pos = pos_slot - s0
                lo = min(max(pos - win // 2, 0), seg)
                hi = min(max(pos + win - win // 2, 0), seg)
                cnt = hi - lo
                for p in range(lo, hi):
                    tin_slot = s0 + p
                    tpar, tin = tin_slot // 128, tin_slot % 128
                    if tpar == par:
                        P[par, wi, tin, to] += 1.0 / cnt
                    else:
                        P[2 + par, wi, tin, to] += 1.0 / cnt
            P[par, wi] -= np.eye(128)
    return np.ascontiguousarray(P.transpose(2, 0, 1, 3)).astype(np.float32)


def shared_inputs(w_in, conv_w, a_log, dt_bias, delta_norm_g, sgu_norm_g, w_spatial, b_spatial, w_pool, pool_scale,
                  w_out, norm1_g, norm2_g, w_mod, b_mod, w_gu, w_down, norm_f):
    f = lambda a: np.ascontiguousarray(np.asarray(a, dtype=np.float32))
    NL = w_in.shape[0]
    cmf, cmb = _masks()
    pv = lambda v, n: f(np.asarray(v).reshape(NL, n, 128).transpose(0, 2, 1))
    wp = np.asarray(w_pool)
    wpl = wp.reshape(NL, 2, 2, 64, 64).transpose(0, 2, 3, 1, 4).reshape(NL, 128, 2, 64)
    return {
        "cmf": cmf, "cmb": cmb,
        "omega": np.ascontiguousarray((1.0 / (10000.0 ** (np.arange(256, dtype=np.float32) / np.float32(256)))).astype(np.float32).reshape(2, 128).T),
        "w_mod": f(w_mod), "b_mod": pv(b_mod, 48), "w_in": f(w_in),
        "conv_w": f(np.asarray(conv_w).reshape(NL, 3, 12, 128).transpose(0, 3, 2, 1)),
        "a_log": f(np.broadcast_to(np.asarray(a_log).reshape(NL, 1, 8), (NL, 128, 8))),
        "dt_bias": f(np.broadcast_to(np.asarray(dt_bias).reshape(NL, 1, 8), (NL, 128, 8))),
        "dng": f(np.broadcast_to(np.tile(np.asarray(delta_norm_g), (1, 4)).reshape(NL, 1, 512), (NL, 128, 512))),
        "sng": f(np.broadcast_to(np.asarray(sgu_norm_g).reshape(NL, 1, 256), (NL, 128, 256))),
        "wsT": f(np.asarray(w_spatial).transpose(0, 3, 1, 2)),
        "bsp": f(np.asarray(b_spatial).reshape(NL, 1, 512)),
        "wpool": f(wpl), "pscale": pv(pool_scale, 2),
        "w_out": f(w_out), "n1g": pv(norm1_g, 8), "n2g": pv(norm2_g, 8),
        "nfg": f(np.asarray(norm_f).reshape(8, 128).T),
        "w_gu": f(w_gu), "w_down": f(w_down),
    }


def core_inputs(x_tok, pe, cvec, s0, chain, seg):
    T = x_tok.shape[0]
    return {
        "xin": np.ascontiguousarray(x_tok, dtype=np.float32),
        "cvec": np.ascontiguousarray(np.asarray(cvec, dtype=np.float32).reshape(8, 128).T),
        "s0in": np.ascontiguousarray(np.asarray(s0, dtype=np.float32).transpose(0, 1, 3, 2, 4)),
        "chain": np.full((128, 1), chain, np.float32),
        "poolm": _pool_mats(seg),
    }


_NC_CACHE = {}


def run(x_prompt, x_sample, state_delta, c, c_ctx, w_in, conv_w, a_log, dt_bias,
        delta_norm_g, sgu_norm_g, w_spatial, b_spatial, w_pool, pool_scale, w_out,
        norm1_g, norm2_g, w_mod, b_mod, w_gu, w_down, norm_f):
    x_prompt = np.asarray(x_prompt, np.float32)
    x_sample = np.asarray(x_sample, np.float32)
    state_delta = np.asarray(state_delta, np.float32)
    c = np.asarray(c, np.float32)
    B, S, _ = x_prompt.shape
    DB, DS, _ = x_sample.shape
    NL = w_in.shape[0]
    T = DS
    NS = T // 256
    shared = shared_inputs(w_in, conv_w, a_log, dt_bias, delta_norm_g, sgu_norm_g, w_spatial, b_spatial, w_pool,
                           pool_scale, w_out, norm1_g, norm2_g, w_mod, b_mod, w_gu, w_down, norm_f)
    n_ctx = 8 - DB
    base, extra = divmod(B, n_ctx)
    counts = [base + (1 if i < extra else 0) for i in range(n_ctx)]
    assert max(counts) <= NS
    zero_s = np.zeros((NL, 2, 4, 128, 128), np.float32)
    in_maps, seq_of = [], []
    start = 0
    for i in range(n_ctx):
        xt = np.zeros((T, D), np.float32)
        n = counts[i]
        xt[:n * S] = x_prompt[start:start + n].reshape(n * S, D)
        seq_of.append((start, n))
        start += n
        m = core_inputs(xt, None, c_ctx, zero_s, 0.0, 256)
        m.update(shared)
        in_maps.append(m)
    for b in range(DB):
        m = core_inputs(x_sample[b], None, c[b], state_delta[b], 1.0, 64)
        m.update(shared)
        in_maps.append(m)
    key = (T, NL)
    if key not in _NC_CACHE:
        _NC_CACHE[key] = build(T, NL)
    res = run_bass_kernel_spmd(_NC_CACHE[key], in_maps, core_ids=list(range(8)))
    y_prompt = np.zeros((B, S, D), np.float32)
    new_state = np.zeros((B, NL, 2, 4, 128, 128), np.float32)
    for i in range(n_ctx):
        s0_, n = seq_of[i]
        r = res.results[i]
        y_prompt[s0_:s0_ + n] = np.asarray(r["yout"])[:n * S].reshape(n, S, D)
        new_state[s0_:s0_ + n] = np.asarray(r["stout"])[:n]
    y_sample = np.stack([np.asarray(res.results[n_ctx + b]["yout"]) for b in range(DB)], axis=0)
    return (y_prompt, y_sample, new_state)


def kernel(x_prompt, x_sample, state_delta, c, c_ctx, w_in, conv_w, a_log, dt_bias,
           delta_norm_g, sgu_norm_g, w_spatial, b_spatial, w_pool, pool_scale, w_out,
           norm1_g, norm2_g, w_mod, b_mod, w_gu, w_down, norm_f):
    return run(x_prompt, x_sample, state_delta, c, c_ctx, w_in, conv_w, a_log, dt_bias,
               delta_norm_g, sgu_norm_g, w_spatial, b_spatial, w_pool, pool_scale, w_out,
               norm1_g, norm2_g, w_mod, b_mod, w_gu, w_down, norm_f)
```

```python
import contextlib
import math
import numpy as np
import concourse.bass as bass
import concourse.mybir as mybir
from concourse.bass_utils import run_bass_kernel_spmd

F32 = mybir.dt.float32
BF16 = mybir.dt.bfloat16
AF = mybir.ActivationFunctionType
ALU = mybir.AluOpType

D = 1024
DC = 8
NCOL = 2832
FF = 2816
FC = 22
EPS = 1e-6
NEG = -30000.0
STRICT = True
INTERLEAVE = True
SKIPB = False
OVERLAP = True
BLIMIT = 10 ** 9
C_Q, C_K, C_V, C_Z, C_B, C_A, C_U, C_VB, C_P = 0, 512, 1024, 1536, 2048, 2056, 2064, 2320, 2576


class Ev:
    __slots__ = ("sem", "sid", "val", "eng")

    def __init__(self, sem, sid, val, eng):
        self.sem, self.sid, self.val, self.eng = sem, sid, val, eng


class TK:
    ROLL = 30000
    NDMA = 8

    def __init__(self, nc, stack):
        self.nc = nc
        self.stack = stack
        self.engs = {"pe": nc.tensor, "act": nc.scalar, "dve": nc.vector, "pool": nc.gpsimd, "sp": nc.sync}
        self.sem = {}
        self.cnt = {}
        self.nsem = 0
        for e in self.engs:
            self._newsem(e)
        self.waited = {e: {} for e in self.engs}
        self.lastw = {}
        self.rd = {}
        self.dpool = {}
        self.dn = {}
        self.last = {}
        self.nwait = 0
        self.nins = 0

    def _alloc(self, name):
        self.nsem += 1
        return self.stack.enter_context(self.nc.semaphore("%s_%d" % (name, self.nsem)))

    def _newsem(self, e):
        self.sem[e] = (self._alloc("e" + e), self.nsem)
        self.cnt[e] = 0

    def _need(self, eng, reads, writes, sreads=()):
        need = {}

        def add(ev, force=False):
            if ev is None or (ev.eng == eng and (eng == "pe" or not STRICT) and not force):
                return
            cur = need.get(ev.sid)
            if cur is None or cur.val < ev.val:
                need[ev.sid] = ev
        for k in reads:
            add(self.lastw.get(k))
            if isinstance(k, str) and k.startswith("ps"):
                for ev in self.rd.get(k, {}).values():
                    add(ev)
        for k in sreads:
            add(self.lastw.get(k), True)
        for k in writes:
            add(self.lastw.get(k))
            for ev in self.rd.get(k, {}).values():
                add(ev)
        return need

    def _dowaits(self, eng, need):
        w = self.waited[eng]
        for sid, ev in need.items():
            if w.get(sid, 0) >= ev.val:
                continue
            self.engs[eng].wait_ge(ev.sem, ev.val)
            w[sid] = ev.val
            self.nwait += 1

    def _record(self, ev, reads, writes):
        for k in writes:
            self.lastw[k] = ev
            self.rd[k] = {}
        for k in reads:
            self.rd.setdefault(k, {})[ev.sid] = ev

    def op(self, eng, fn, reads=(), writes=(), carrier=None, sreads=()):
        n0 = self.nwait
        self._dowaits(eng, self._need(eng, reads, writes, sreads))
        if carrier is not None and self.nwait > n0:
            carrier()
        ins = fn()
        if isinstance(ins, (list, tuple)):
            ins = ins[-1]
        if self.cnt[eng] >= self.ROLL:
            self._newsem(eng)
        s, sid = self.sem[eng]
        self.cnt[eng] += 1
        ins.then_inc(s, 1)
        ev = Ev(s, sid, self.cnt[eng], eng)
        self.last[eng] = ev
        self._record(ev, list(reads) + list(sreads), writes)
        self.nins += 1
        return ev

    def dma(self, q, fn, reads=(), writes=()):
        need = self._need("dma", reads, writes)
        if q not in self.dpool:
            self.dpool[q] = [[self._alloc("d" + q), self.nsem, 0] for _ in range(self.NDMA)]
            self.dn[q] = 0
        slot = self.dpool[q][self.dn[q] % self.NDMA]
        self.dn[q] += 1
        if slot[2] > 0:
            ev0 = Ev(slot[0], slot[1], slot[2], "dma")
            cur = need.get(ev0.sid)
            if cur is None or cur.val < ev0.val:
                need[ev0.sid] = ev0
        self._dowaits(q, need)
        ins = fn()
        slot[2] += 16
        ins.then_inc(slot[0], 16)
        ev = Ev(slot[0], slot[1], slot[2], "dma")
        self._record(ev, reads, writes)
        return ev

    def wait_all(self, eng):
        need = {}
        for e in self.engs:
            if e == eng or e not in self.last:
                continue
            ev = self.last[e]
            need[ev.sid] = ev
        for q, slots in self.dpool.items():
            for s, sid, val in slots:
                if val > 0:
                    need[sid] = Ev(s, sid, val, "dma")
        self._dowaits(eng, need)

    def barrier(self):
        for e in self.engs:
            self.wait_all(e)
        self.lastw = {}
        self.rd = {}


def bcast(ap, pos, n):
    l = [list(x) for x in ap.ap]
    l.insert(pos + 1, [0, n])
    return bass.AP(tensor=ap.tensor, offset=ap.offset, ap=l)


class PSA:
    def __init__(self):
        self.free = list(range(8))

    def get(self, n=1):
        if n == 1:
            return self.free.pop(0)
        for b in self.free:
            if b % 2 == 0 and (b + 1) in self.free:
                self.free.remove(b)
                self.free.remove(b + 1)
                return b
        raise RuntimeError("no psum pair")

    def put(self, b, n=1):
        for i in range(n):
            self.free.append(b + i)


def build(T=2048, NL=2, dbg=None):
    NS, NT, NG = T // 256, T // 128, T // 512
    HT = min(1024, T)
    NHALF = T // HT
    NGH = HT // 512
    nc = bass.Bass("TRN2", target_bir_lowering=False)

    def din(name, shape):
        return nc.dram_tensor(name, list(shape), F32, kind="ExternalInput").ap()

    xin = din("xin", [T, D])
    omega_in = din("omega", [128, 2])
    cvec = din("cvec", [128, 8])
    s0in = din("s0in", [NL, 2, 128, 4, 128])
    chain_in = din("chain", [128, 1])
    poolm_in = din("poolm", [128, 4, 4, 128])
    w_mod = din("w_mod", [NL, D, 6 * D])
    b_mod = din("b_mod", [NL, 128, 48])
    w_in = din("w_in", [NL, D, NCOL])
    conv_w = din("conv_w", [NL, 128, 12, 3])
    a_log = din("a_log", [NL, 128, 8])
    dt_bias = din("dt_bias", [NL, 128, 8])
    dng = din("dng", [NL, 128, 512])
    sng = din("sng", [NL, 128, 256])
    wsT = din("wsT", [NL, 128, 4, 128])
    bsp = din("bsp", [NL, 1, 512])
    wpool = din("wpool", [NL, 128, 2, 64])
    pscale = din("pscale", [NL, 128, 2])
    w_out = din("w_out", [NL, D, D])
    n1g = din("n1g", [NL, 128, 8])
    n2g = din("n2g", [NL, 128, 8])
    nfg = din("nfg", [128, 8])
    w_gu = din("w_gu", [NL, D, 2 * FF])
    w_down = din("w_down", [NL, FF, D])
    cmf_in = din("cmf", [128, 6, 128])
    cmb_in = din("cmb", [128, 9, 128])
    yout = nc.dram_tensor("yout", [T, D], F32, kind="ExternalOutput").ap()
    stout = nc.dram_tensor("stout", [NS, NL, 2, 4, 128, 128], F32, kind="ExternalOutput").ap()
    dbg_out = {}
    if dbg:
        for name, shape in dbg.items():
            dbg_out[name] = nc.dram_tensor("dbg_" + name, list(shape), F32, kind="ExternalOutput").ap()

    with contextlib.ExitStack() as st:
        tk = TK(nc, st)
        psa = PSA()

        def sb(name, shape, dt):
            return st.enter_context(nc.sbuf_tensor(name, list(shape), dt))

        PS = st.enter_context(nc.psum_tensor("PS", [128, 8, 512], F32))

        def pk(b):
            return "ps%d" % b

        V, Sc, Pl, Pe = nc.vector, nc.scalar, nc.gpsimd, nc.tensor

        xT = sb("xT", [128, DC, T], F32)
        cm = sb("cm_sb", [128, 6, 128], F32)
        cmb = sb("cmb_sb", [128, 9, 128], BF16)
        onesb = sb("onesb", [128, 128], BF16)
        onesf = sb("onesf", [128, 128], F32)
        chain = sb("chain_sb", [128, 1], F32)
        csil = sb("csil", [128, 8], F32)
        modv = sb("modv", [128, NL, 48], F32)
        gm1 = sb("gm1", [128, NL, 8], F32)
        gm2 = sb("gm2", [128, NL, 8], F32)
        nfg_sb = sb("nfg_sb", [128, 8], F32)
        epst = sb("epst", [128, 4], F32)
        lsm = sb("lsm", [128, 64], F32)
        convw = sb("convw", [128, 12, 3], F32)
        bmod_sb = sb("bmod_sb", [128, 48], F32)
        NAR = (132096 + 7168) // 2
        arena = sb("arena", [128, NAR], BF16)

        def carve(off, shape, dt):
            n = int(np.prod(shape))
            nb = n * (4 if dt == F32 else 2)
            assert off % 4 == 0 and off + nb <= NAR * 2, (off, nb, NAR * 2)
            a = arena[:, off // 2: (off + nb) // 2]
            if dt == F32:
                a = a.bitcast(F32)
            if len(shape) == 2:
                return a.rearrange("p (a b) -> p a b", a=shape[0]), off + nb
            if len(shape) == 3:
                return a.rearrange("p (a b c) -> p a b c", a=shape[0], b=shape[1]), off + nb
            return a, off + nb

        IDENT, TRI_F, TRI_B, BLK, NEG_F, NEG_B = 0, 1, 2, 3, 4, 5
        STR_F, LM0 = 1, 3

        tk.dma("sp", lambda: nc.sync.dma_start(out=cm[:], in_=cmf_in), writes=["cm"])
        tk.dma("pool", lambda: Pl.dma_start(out=cmb[:], in_=cmb_in), writes=["cmb"])
        tk.dma("sp", lambda: nc.sync.dma_start(out=chain[:], in_=chain_in), writes=["chain"])
        tk.dma("sp", lambda: nc.sync.dma_start(out=csil[:], in_=cvec), writes=["csil"])
        tk.dma("sp", lambda: nc.sync.dma_start(out=nfg_sb[:], in_=nfg), writes=["nfg"])
        tk.op("dve", lambda: V.memset(onesf[:], 1.0), writes=["onesf"])
        tk.op("dve", lambda: V.memset(onesb[:], 1.0), writes=["onesb"])
        tk.op("dve", lambda: V.memset(epst[:, 0:1], EPS), writes=["epst"])
        tk.op("dve", lambda: V.memset(epst[:, 1:2], 128.0 * EPS), writes=["epst"])
        tk.op("dve", lambda: V.memset(epst[:, 2:3], 1.0), writes=["epst"])
        tk.op("dve", lambda: V.memset(epst[:, 3:4], 0.0), writes=["epst"])
        tk.op("act", lambda: Sc.activation(out=csil[:], in_=csil[:], func=AF.Silu), reads=["csil"], writes=["csil"])
        junk = sb("junk", [128, 4], F32)
        tk.op("act", lambda: Sc.copy(out=junk[:, 0:2], in_=epst[:, 0:2]), reads=["epst"])
        actc = lambda: Sc.copy(out=junk[:, 2:3], in_=junk[:, 0:1])
        identf = cm[:, IDENT, :]
        identb = cmb[:, IDENT, :]

        off = 0
        xtk, off = carve(off, [2, 1, D], F32)
        for t in range(NT):
            i = t % 2
            tk.dma("sp", lambda t=t, i=i: nc.sync.dma_start(out=xtk[:, i, 0, :], in_=xin[t * 128:(t + 1) * 128, :]), writes=[("xtk", i)])
            for hh in range(2):
                b = psa.get()
                tk.op("pe", lambda i=i, hh=hh, b=b: [Pe.transpose(out=PS[:, b, j * 128:(j + 1) * 128], in_=xtk[:, i, 0, (hh * 4 + j) * 128:(hh * 4 + j + 1) * 128], identity=identf) for j in range(4)],
                      reads=[("xtk", i), "cm"], writes=[pk(b)])
                eng = "act" if hh == 0 else "dve"
                if eng == "act":
                    tk.op("act", lambda t=t, hh=hh, b=b: Sc.copy(out=xT[:, hh * 4:(hh + 1) * 4, t * 128:(t + 1) * 128], in_=PS[:, b, :].rearrange("p (j n) -> p j n", j=4)),
                          reads=[pk(b)], writes=[("xT", t // 4)])
                else:
                    tk.op("dve", lambda t=t, hh=hh, b=b: V.tensor_copy(out=xT[:, hh * 4:(hh + 1) * 4, t * 128:(t + 1) * 128], in_=PS[:, b, :].rearrange("p (j n) -> p j n", j=4)),
                          reads=[pk(b)], writes=[("xT", t // 4)])
                psa.put(b)

        I32 = mybir.dt.int32
        NR = T // 64
        omg, off = carve(off, [1, 2], F32)
        posr, off = carve(off, [1, 64], F32)
        ptab, off = carve(off, [DC, 64], F32)
        pki_, off = carve(off, [DC, 64], F32)
        pki = pki_[:, :, :].rearrange("p a n -> p (a n)").bitcast(I32).rearrange("p (a n) -> p a n", a=DC)
        tk.dma("sp", lambda: nc.sync.dma_start(out=omg[:, 0, :], in_=omega_in), writes=["omg"])
        tk.op("pool", lambda: Pl.iota(posr[:, 0, :], pattern=[[1, 64]], base=0, channel_multiplier=0, allow_small_or_imprecise_dtypes=True), writes=["posr"])
        for c in range(DC):
            ph = 0.0 if (c % 4) < 2 else math.pi / 2
            tk.op("dve", lambda c=c, ph=ph: V.tensor_scalar(out=ptab[:, c, :], in0=posr[:, 0, :], scalar1=omg[:, 0, c % 2:c % 2 + 1], scalar2=ph, op0=ALU.mult, op1=ALU.add),
                  reads=["posr"], sreads=["omg"], writes=["ptab"])
        tk.op("dve", lambda: V.tensor_scalar(out=pki, in0=ptab[:, :, :], scalar1=1.0 / (2 * math.pi), scalar2=None, op0=ALU.mult), reads=["ptab"], writes=["pki"])
        tk.op("dve", lambda: V.scalar_tensor_tensor(out=ptab[:, :, :], in0=pki, scalar=-2 * math.pi, in1=ptab[:, :, :], op0=ALU.mult, op1=ALU.add), reads=["pki", "ptab"], writes=["ptab"])
        tk.op("act", lambda: Sc.activation(out=ptab[:, :, :], in_=ptab[:, :, :], func=AF.Sin), reads=["ptab"], writes=["ptab"])
        for g in range(NG):
            r0 = g * 8
            for c in range(DC):
                if c < 4:
                    src = bcast(ptab[:, c, r0:r0 + 8], 1, 64)
                else:
                    src = bcast(ptab[:, c, 0:64], 0, 8)
                tk.op("dve", lambda c=c, src=src, g=g: V.scalar_tensor_tensor(out=xT[:, c, g * 512:(g + 1) * 512].rearrange("p (r n) -> p r n", n=64), in0=src, scalar=chain[:, 0:1],
                                                                           in1=xT[:, c, g * 512:(g + 1) * 512].rearrange("p (r n) -> p r n", n=64), op0=ALU.mult, op1=ALU.add),
                      reads=["ptab", ("xT", g)], sreads=["chain"], writes=[("xT", g)])

        wmq, off = carve(off, [4 * 8, 512], BF16)
        modrow, off = carve(off, [1, 6 * D], F32)
        csb, off = carve(off, [1, 8], BF16)
        tk.op("dve", lambda: V.tensor_copy(out=csb[:, 0, :], in_=csil[:]), reads=["csil"], writes=["csb"])
        for l in range(NL):
            tk.dma("sp", lambda l=l: nc.sync.dma_start(out=bmod_sb[:], in_=b_mod[l]), writes=["bmod"])
            for cb in range(12):
                i = cb % 4
                tk.dma("pool", lambda l=l, cb=cb, i=i: Pl.dma_start(out=wmq[:, i * 8:(i + 1) * 8, :], in_=w_mod[l, :, cb * 512:(cb + 1) * 512].rearrange("(kc p) n -> p kc n", p=128)),
                       writes=[("wmq", i)])
                b = psa.get()
                tk.op("pe", lambda i=i, b=b: [Pe.matmul(PS[0:1, b, :], lhsT=csb[:, 0, kc:kc + 1], rhs=wmq[:, i * 8 + kc, :], start=(kc == 0), stop=(kc == 7)) for kc in range(8)],
                      reads=[("wmq", i), "csb"], writes=[pk(b)])
                tk.op("act", lambda cb=cb, b=b: Sc.copy(out=modrow[0:1, 0, cb * 512:(cb + 1) * 512], in_=PS[0:1, b, :]), reads=[pk(b)], writes=["modrow"])
                psa.put(b)
            b = psa.get()
            tk.op("pe", lambda b=b: [Pe.matmul(PS[:, b, j:j + 1], lhsT=modrow[0:1, 0, j * 128:(j + 1) * 128], rhs=onesf[0:1, 0:1], start=True, stop=True) for j in range(48)],
                  reads=["modrow", "onesf"], writes=[pk(b)])
            tk.op("dve", lambda l=l, b=b: V.tensor_tensor(out=modv[:, l, :], in0=PS[:, b, 0:48], in1=bmod_sb[:], op=ALU.add), reads=[pk(b), "bmod"], writes=["modv"])
            psa.put(b)
            tk.dma("sp", lambda l=l: nc.sync.dma_start(out=gm1[:, l, :], in_=n1g[l]), writes=["gm1"])
            tk.dma("sp", lambda l=l: nc.sync.dma_start(out=gm2[:, l, :], in_=n2g[l]), writes=["gm2"])
            tk.op("dve", lambda l=l: V.scalar_tensor_tensor(out=gm1[:, l, :], in0=modv[:, l, 8:16], scalar=1.0, in1=gm1[:, l, :], op0=ALU.add, op1=ALU.mult), reads=["modv", "gm1"], writes=["gm1"])
            tk.op("dve", lambda l=l: V.scalar_tensor_tensor(out=gm2[:, l, :], in0=modv[:, l, 32:40], scalar=1.0, in1=gm2[:, l, :], op0=ALU.add, op1=ALU.mult), reads=["modv", "gm2"], writes=["gm2"])
        tk.barrier()

        def rsqrt_psum(b, rs, rinv, scale, eps_ap):
            tk.op("act", lambda: Sc.activation(out=rs[:, 0, :], in_=PS[:, b, :], func=AF.Ln, bias=eps_ap, scale=scale), reads=[pk(b), "epst"], writes=["rs"])
            tk.op("act", lambda: Sc.activation(out=rinv[:, 0, :], in_=rs[:, 0, :], func=AF.Exp, scale=-0.5), reads=["rs"], writes=["rinv"])

        def rms_rinv_gen(cols, nfeat_scale, sqb, rs, rinv, xkeys):
            b = psa.get()
            for c in range(DC):
                i = c % 2
                tk.op("act", lambda c=c, i=i: Sc.activation(out=sqb[:, i, :], in_=xT[:, c, cols], func=AF.Square), reads=xkeys, writes=[("sqb", i)])
                tk.op("pe", lambda c=c, i=i, b=b: Pe.matmul(PS[:, b, :], lhsT=onesb[:], rhs=sqb[:, i, :], start=(c == 0), stop=(c == DC - 1)), reads=[("sqb", i), "onesb"], writes=[pk(b)])
                yield
            rsqrt_psum(b, rs, rinv, nfeat_scale, epst[:, 0:1])
            psa.put(b)
            yield

        def rms_rinv(*a):
            for _ in rms_rinv_gen(*a):
                pass

        def norm_mod_gen(cols, gmv, shv, hT, hcols, hkey, sqb, rs, rinv, tmp, xkeys):
            yield from rms_rinv_gen(cols, 1.0 / D, sqb, rs, rinv, xkeys)
            ntmp = tmp.shape[1]
            for c in range(DC):
                i = c % ntmp
                tk.op("dve", lambda c=c, i=i: V.tensor_tensor(out=tmp[:, i, :], in0=xT[:, c, cols], in1=rinv[:, 0, :], op=ALU.mult), reads=xkeys + ["rinv"], writes=[("tmp", i)])
                tk.op("act", lambda c=c, i=i: Sc.activation(out=hT[:, c, hcols], in_=tmp[:, i, :], func=AF.Identity, bias=shv[:, c:c + 1], scale=gmv[:, c:c + 1]),
                      reads=[("tmp", i), "modv", "gm1", "gm2"], writes=[hkey])
                yield

        def norm_mod(*a):
            for _ in norm_mod_gen(*a):
                pass

        def interleave(*gens):
            gens = [g for g in gens if g is not None]
            alive = [True] * len(gens)
            while any(alive):
                for i_, g_ in enumerate(gens):
                    if alive[i_]:
                        try:
                            next(g_)
                        except StopIteration:
                            alive[i_] = False

        def wload(dst, key, src):
            tk.dma("pool", lambda: Pl.dma_start(out=dst, in_=src), writes=[key])

        def dump(name, ap, keys):
            if name in dbg_out:
                tk.dma("pool", lambda: Pl.dma_start(out=dbg_out[name], in_=ap), reads=keys)

        for l in range(NL):
            sh1, g1v = modv[:, l, 0:8], modv[:, l, 16:24]
            sh2, g2v = modv[:, l, 24:32], modv[:, l, 40:48]
            off = 0
            sz, off = carve(off, [NT, 512], BF16)
            beta, off = carve(off, [NT, 8], F32)
            gl, off = carve(off, [NT, 8], F32)
            qkvT, off_qkv = carve(off, [12, T], BF16)
            hT, off_x = carve(off_qkv, [DC, T], BF16)
            tk.dma("sp", lambda l=l: nc.sync.dma_start(out=lsm[:, 0:8], in_=a_log[l]), writes=["lsm"])
            tk.dma("sp", lambda l=l: nc.sync.dma_start(out=lsm[:, 8:16], in_=dt_bias[l]), writes=["lsm"])
            tk.dma("sp", lambda l=l: nc.sync.dma_start(out=lsm[:, 16:18], in_=pscale[l]), writes=["lsm"])
            tk.dma("sp", lambda l=l: nc.sync.dma_start(out=convw[:], in_=conv_w[l]), writes=["convw"])
            tk.op("act", lambda: Sc.activation(out=lsm[:, 0:8], in_=lsm[:, 0:8], func=AF.Exp), reads=["lsm"], writes=["lsm"])
            ea, dtb, psc = lsm[:, 0:8], lsm[:, 8:16], lsm[:, 16:18]

            regs = [[off, off_qkv], [off_x, NAR * 2]]

            def c2(shape, dt):
                nb = int(np.prod(shape)) * (4 if dt == F32 else 2)
                for r in regs:
                    if r[0] + nb <= r[1]:
                        v, r[0] = carve(r[0], shape, dt)
                        return v
                raise RuntimeError("arena full")
            wz = c2([8, 512], BF16)
            wvp = c2([8, 512], BF16)
            wu = c2([8, 256], BF16)
            wba = c2([8, 16], BF16)
            woB = c2([4, D], BF16)
            dngb = c2([1, 512], F32)
            sngb = c2([1, 256], F32)
            sqb = c2([2, 512], BF16)
            rs = c2([1, 512], F32)
            rinv = c2([1, 512], F32)
            tmp = c2([2, 512], F32)
            guT = c2([2 * 2, 512], BF16)
            ybyc = c2([2 * 4, 512], BF16)
            szt = c2([2, 512], F32)
            vg = c2([2 * 4, 256], F32)
            vn = c2([2 * 4, 256], BF16)
            vsq = c2([2, 256], F32)
            ssv = c2([2 * 4, 8], F32)
            ptok = c2([2 * 4, 256], BF16)
            dmT = c2([2 * 2, 256], BF16)
            poolm = c2([16, 128], BF16).rearrange("p (x w) n -> p x w n", x=4)
            wsT_sb = c2([4, 128], BF16)
            wp_sb = c2([2, 64], BF16)
            bsp_sb = c2([1, 512], F32)
            bal = c2([NT, 16], F32)
            wload(poolm, "poolm", poolm_in)
            wload(wsT_sb, "wsT", wsT[l])
            wload(wp_sb, "wp", wpool[l])
            tk.dma("sp", lambda l=l: nc.sync.dma_start(out=bsp_sb[0:1, 0, :], in_=bsp[l]), writes=["bsp"])
            w_in_l = w_in[l].rearrange("(kc p) n -> p kc n", p=128)
            wload(wz[:], "wz", w_in_l[:, :, C_Z:C_Z + 512])
            wload(wvp[:], "wvp", w_in_l[:, :, C_VB:C_VB + 512])
            wload(wu[:], "wu", w_in_l[:, :, C_U:C_U + 256])
            wload(wba[:], "wba", w_in_l[:, :, C_B:C_B + 16])
            wload(woB[:], "woB", w_out[l, 512:1024, :].rearrange("(kc p) n -> p kc n", p=128))
            tk.dma("sp", lambda l=l: nc.sync.dma_start(out=dngb[:, 0, :], in_=dng[l]), writes=["dngb"])
            tk.dma("sp", lambda l=l: nc.sync.dma_start(out=sngb[:, 0, :], in_=sng[l]), writes=["sngb"])
            def a1_norm(g):
                cols = slice(g * 512, (g + 1) * 512)
                return norm_mod_gen(cols, gm1[:, l, :], sh1, hT, cols, ("hT", g), sqb, rs, rinv, tmp, [("xT", g)])

            def a1_group(g):
                cols = slice(g * 512, (g + 1) * 512)
                tcs = [slice((g * 4 + tt) * 128, (g * 4 + tt + 1) * 128) for tt in range(4)]
                S_ = g % 2
                guT_ = guT[:, S_ * 2:S_ * 2 + 2, :]
                ybyc_ = ybyc[:, S_ * 4:S_ * 4 + 4, :]
                vg_ = vg[:, S_ * 4:S_ * 4 + 4, :]
                vn_ = vn[:, S_ * 4:S_ * 4 + 4, :]
                ssv_ = ssv[:, S_ * 4:S_ * 4 + 4, :]
                ptok_ = ptok[:, S_ * 4:S_ * 4 + 4, :]
                for uc in range(2):
                    b = psa.get()
                    tk.op("pe", lambda uc=uc, b=b: [Pe.matmul(PS[:, b, :], lhsT=wu[:, kc, uc * 128:(uc + 1) * 128], rhs=hT[:, kc, cols], start=(kc == 0), stop=(kc == 7)) for kc in range(8)],
                          reads=["wu", ("hT", g)], writes=[pk(b)])
                    tk.op("act", lambda uc=uc, b=b: Sc.activation(out=guT_[:, uc, :], in_=PS[:, b, :], func=AF.Gelu_apprx_tanh), reads=[pk(b)], writes=[("guT", S_)])
                    psa.put(b)
                    yield
                for tt in range(4):
                    b = psa.get()
                    tk.op("pe", lambda b=b, tt=tt: [Pe.matmul(PS[:, b, :], lhsT=hT[:, kc, tcs[tt]], rhs=wvp[:, kc, :], start=(kc == 0), stop=(kc == 7)) for kc in range(8)],
                          reads=["wvp", ("hT", g)], writes=[pk(b)])
                    tk.op("act", lambda b=b, tt=tt: Sc.activation(out=vg_[:, tt, :], in_=PS[:, b, 0:256], func=AF.Gelu_apprx_tanh), reads=[pk(b)], writes=[("vg", S_, tt)])
                    tk.op("dve", lambda b=b, tt=tt: V.tensor_copy(out=ptok_[:, tt, :], in_=PS[:, b, 256:512]), reads=[pk(b)], writes=[("ptok", S_, tt)])
                    psa.put(b)
                    yield
                    i = tt % 2
                    tk.op("pool", lambda tt=tt, i=i: Pl.tensor_tensor(out=vsq[:, i, :], in0=vg_[:, tt, :], in1=vg_[:, tt, :], op=ALU.mult), reads=[("vg", S_, tt)], writes=[("vsq", i)])
                    tk.op("dve", lambda tt=tt, i=i: V.tensor_reduce(out=ssv_[:, tt, 0:4], in_=vsq[:, i, :].rearrange("p (g c) -> p g c", g=4), axis=mybir.AxisListType.X, op=ALU.add),
                          reads=[("vsq", i)], writes=[("ssv", S_, tt)])
                    yield
                for tt in range(4):
                    t = g * 4 + tt
                    i = tt % 2
                    b = psa.get()
                    tk.op("pe", lambda b=b, tt=tt: [Pe.matmul(PS[:, b, :], lhsT=hT[:, kc, tcs[tt]], rhs=wz[:, kc, :], start=(kc == 0), stop=(kc == 7)) for kc in range(8)],
                          reads=["wz", ("hT", g)], writes=[pk(b)])
                    tk.op("act", lambda b=b, i=i: Sc.activation(out=szt[:, i, :], in_=PS[:, b, :], func=AF.Silu), reads=[pk(b)], writes=[("szt", i)])
                    psa.put(b)
                    tk.op("pool", lambda t=t, i=i: Pl.tensor_tensor(out=sz[:, t, :], in0=szt[:, i, :], in1=dngb[:, 0, :], op=ALU.mult), reads=[("szt", i), "dngb"], writes=[("sz", t)])
                    yield
                for tt in range(4):
                    t = g * 4 + tt
                    tk.op("act", lambda tt=tt: Sc.activation(out=ssv_[:, tt, 0:4], in_=ssv_[:, tt, 0:4], func=AF.Sqrt, bias=epst[:, 0:1], scale=1.0 / 64), reads=[("ssv", S_, tt), "epst"], writes=[("ssv", S_, tt)])
                    tk.op("dve", lambda tt=tt: V.reciprocal(out=ssv_[:, tt, 4:8], in_=ssv_[:, tt, 0:4]), reads=[("ssv", S_, tt)], writes=[("ssv", S_, tt)])
                    for gq in range(4):
                        tk.op("dve", lambda gq=gq, tt=tt: V.scalar_tensor_tensor(out=vn_[:, tt, gq * 64:(gq + 1) * 64], in0=vg_[:, tt, gq * 64:(gq + 1) * 64], scalar=ssv_[:, tt, 4 + gq:5 + gq],
                                                                                  in1=sngb[:, 0, gq * 64:(gq + 1) * 64], op0=ALU.mult, op1=ALU.mult),
                              reads=[("vg", S_, tt), "sngb"], sreads=[("ssv", S_, tt)], writes=[("vn", S_, tt, gq)])
                    b = psa.get()
                    tk.op("pe", lambda b=b, tt=tt: [Pe.matmul(PS[:, b, 0:16], lhsT=hT[:, kc, tcs[tt]], rhs=wba[:, kc, :], start=(kc == 0), stop=(kc == 7)) for kc in range(8)],
                          reads=["wba", ("hT", g)], writes=[pk(b)])
                    tk.op("dve", lambda b=b, t=t: V.tensor_copy(out=bal[:, t, :], in_=PS[:, b, 0:16]), reads=[pk(b)], writes=[("bal", t)])
                    psa.put(b)
                    yield
                for tt in range(4):
                    tl = slice(tt * 128, (tt + 1) * 128)
                    b = psa.get()

                    def sgu(b=b, tt=tt):
                        r = []
                        for gq in range(4):
                            o = PS[(gq % 2) * 64:(gq % 2) * 64 + 64, b, (gq // 2) * 128:(gq // 2) * 128 + 128]
                            r.append(Pe.matmul(o, lhsT=vn_[:, tt, gq * 64:(gq + 1) * 64], rhs=wsT_sb[:, gq, :], start=True, stop=False))
                            r.append(Pe.matmul(o, lhsT=onesf[0:1, 0:64], rhs=bsp_sb[0:1, 0, gq * 128:(gq + 1) * 128], start=False, stop=True))
                        return r
                    tk.op("pe", sgu, reads=[("vn", S_, tt, gq) for gq in range(4)] + ["wsT", "bsp", "onesf"], writes=[pk(b)])
                    tk.op("dve", lambda b=b, tl=tl: V.tensor_tensor(out=ybyc_[:, 0:2, tl], in0=PS[:, b, 0:256].rearrange("p (a n) -> p a n", a=2), in1=guT_[:, :, tl], op=ALU.mult),
                          reads=[pk(b), ("guT", S_)], writes=[("ybyc", S_, 0)])
                    psa.put(b)
                    yield
                for a in range(4):
                    al_ = slice(a * 128, a * 128 + 128)
                    b = psa.get()

                    def pm(b=b, a=a):
                        r = []
                        for w in range(4):
                            o = PS[(w % 2) * 64:(w % 2) * 64 + 64, b, (w // 2) * 128:(w // 2) * 128 + 128]
                            r.append(Pe.matmul(o, lhsT=ptok_[:, a, w * 64:(w + 1) * 64], rhs=poolm[:, a % 2, w, :], start=True, stop=False))
                            r.append(Pe.matmul(o, lhsT=ptok_[:, a ^ 1, w * 64:(w + 1) * 64], rhs=poolm[:, 2 + a % 2, w, :], start=False, stop=True))
                        return r
                    tk.op("pe", pm, reads=[("ptok", S_, a), ("ptok", S_, a ^ 1), "poolm"], writes=[pk(b)])
                    i = S_ * 2 + a % 2
                    tk.op("act", lambda b=b, i=i: Sc.copy(out=dmT[:, i, :], in_=PS[:, b, 0:256]), reads=[pk(b)], writes=[("dmT", i)])
                    psa.put(b)
                    yield
                    b = psa.get()

                    def pm2(b=b, i=i):
                        r = []
                        for w in range(4):
                            pr = slice((w % 2) * 64, (w % 2) * 64 + 64)
                            r.append(Pe.matmul(PS[pr, b, (w // 2) * 128:(w // 2) * 128 + 128], lhsT=wp_sb[pr, w // 2, :], rhs=dmT[pr, i, (w // 2) * 128:(w // 2) * 128 + 128], start=True, stop=True))
                        return r
                    tk.op("pe", pm2, reads=[("dmT", i), "wp"], writes=[pk(b)])
                    tk.op("act", lambda b=b, al_=al_: Sc.activation(out=ybyc_[:, 2:4, al_], in_=PS[:, b, 0:256].rearrange("p (a n) -> p a n", a=2), func=AF.Identity, scale=1.0),
                          reads=[pk(b)], writes=[("ybyc", S_, 1)])
                    psa.put(b)
                    yield
                for ch in range(2):
                    tk.op("pool", lambda ch=ch: Pl.tensor_scalar(out=ybyc_[:, 2 + ch, :], in0=ybyc_[:, 2 + ch, :], scalar1=psc[:, ch:ch + 1], scalar2=None, op0=ALU.mult),
                          reads=[("ybyc", S_, 1)], sreads=["lsm"], writes=[("ybyc", S_, 1)])
                if g == 0:
                    dump("ybyc%d" % l, ybyc_[:, :, :], [("ybyc", S_, 0), ("ybyc", S_, 1)])
                yield
                for dc in range(DC):
                    b = psa.get()
                    tk.op("pe", lambda b=b, dc=dc: [Pe.matmul(PS[:, b, :], lhsT=woB[:, kc, dc * 128:(dc + 1) * 128], rhs=ybyc_[:, kc, :], start=(kc == 0), stop=(kc == 3)) for kc in range(4)],
                          reads=["woB", ("ybyc", S_, 0), ("ybyc", S_, 1)], writes=[pk(b)])
                    tk.op("dve", lambda b=b, dc=dc: V.scalar_tensor_tensor(out=xT[:, dc, cols], in0=PS[:, b, :], scalar=g1v[:, dc:dc + 1], in1=xT[:, dc, cols], op0=ALU.mult, op1=ALU.add),
                          reads=[pk(b), "modv", ("xT", g)], writes=[("xT", g)])
                    psa.put(b)
                    yield

            ndone = [False] * NG

            def normw(g):
                yield from a1_norm(g)
                ndone[g] = True

            def groupw(g):
                while not ndone[g]:
                    yield
                yield from a1_group(g)

            def gchain(*gs):
                for g_ in gs:
                    yield from g_
            interleave(gchain(*[normw(g) for g in range(NG)]),
                       gchain(*[groupw(g) for g in range(0, NG, 2)]),
                       gchain(*[groupw(g) for g in range(1, NG, 2)]))
            tk.op("act", lambda: Sc.activation(out=beta[:, :, :], in_=bal[:, :, 0:8], func=AF.Sigmoid), reads=[("bal", t) for t in range(NT)], writes=["beta"])
            tk.op("dve", lambda: V.tensor_tensor(out=gl[:, :, :], in0=bal[:, :, 8:16], in1=bcast(dtb, 0, NT), op=ALU.add), reads=[("bal", t) for t in range(NT)] + ["lsm"], writes=["gl"])
            tk.op("act", lambda: Sc.activation(out=gl[:, :, :], in_=gl[:, :, :], func=AF.Exp), reads=["gl"], writes=["gl"])
            tk.op("act", lambda: Sc.activation(out=gl[:, :, :], in_=gl[:, :, :], func=AF.Ln, bias=epst[:, 2:3], scale=1.0), reads=["gl", "epst"], writes=["gl"])
            tk.op("dve", lambda: V.scalar_tensor_tensor(out=gl[:, :, :], in0=gl[:, :, :], scalar=-1.0, in1=bcast(ea, 0, NT), op0=ALU.mult, op1=ALU.mult), reads=["gl", "lsm"], writes=["gl"])
            dump("beta%d" % l, beta, ["beta"])
            dump("gl%d" % l, gl, ["gl"])
            dump("sz%d" % l, sz[:, :, :], [("sz", t) for t in range(NT)])
            tk.barrier()

            o2 = off_x
            raw, o2 = carve(o2, [2 * NS, 258], F32)
            acc, o2 = carve(o2, [NS, 256], F32)
            wch, o2 = carve(o2, [3 * 8, 128], BF16)
            sqb, o2 = carve(o2, [2, 512], BF16)
            rs2, o2 = carve(o2, [2, 512], F32)
            for i in range(2):
                tk.op("dve", lambda i=i: V.memset(raw[:, i * NS, 0:1], 0.0), writes=[("raw", i)])
                tk.op("dve", lambda i=i: V.memset(raw[:, i * NS + NS - 1, 257:258], 0.0), writes=[("raw", i)])

            def wch_load(ci):
                j = ci % 3
                wload(wch[:, j * 8:(j + 1) * 8, :], ("wch", j), w_in_l[:, :, ci * 128:(ci + 1) * 128])
            wch_load(0)
            wch_load(1)
            for ci in range(12):
                j = ci % 3
                i = ci % 2
                if ci + 2 < 12:
                    wch_load(ci + 2)
                rw = raw[:, i * NS:(i + 1) * NS, :]
                for g in range(NG):
                    b = psa.get()
                    tk.op("pe", lambda b=b, g=g, j=j: [Pe.matmul(PS[:, b, :], lhsT=wch[:, j * 8 + kc, :], rhs=hT[:, kc, g * 512:(g + 1) * 512], start=(kc == 0), stop=(kc == 7)) for kc in range(8)],
                          reads=[("wch", j), ("hT", g)], writes=[pk(b)])
                    tk.op("act", lambda b=b, g=g, rw=rw: Sc.copy(out=rw[:, 2 * g:2 * g + 2, 1:257], in_=PS[:, b, :].rearrange("p (s n) -> p s n", s=2)), reads=[pk(b)], writes=[("raw", i)])
                    psa.put(b)
                if NS > 1:
                    tk.op("dve", lambda rw=rw: V.tensor_scalar(out=rw[:, 1:NS, 0:1], in0=rw[:, 0:NS - 1, 256:257], scalar1=chain[:, 0:1], scalar2=None, op0=ALU.mult), reads=[("raw", i), "chain"], writes=[("raw", i)])
                    tk.op("dve", lambda rw=rw: V.tensor_scalar(out=rw[:, 0:NS - 1, 257:258], in0=rw[:, 1:NS, 1:2], scalar1=chain[:, 0:1], scalar2=None, op0=ALU.mult), reads=[("raw", i), "chain"], writes=[("raw", i)])
                for tap in (1, 0, 2):
                    for g in range(NG):
                        ag = acc[:, 2 * g:2 * g + 2, :]
                        rg = rw[:, 2 * g:2 * g + 2, :]
                        ka = ("acc", g)
                        if tap == 1:
                            tk.op("dve", lambda rg=rg, ag=ag, ci=ci: V.tensor_scalar(out=ag, in0=rg[:, :, 1:257], scalar1=convw[:, ci, 1:2], scalar2=None, op0=ALU.mult), reads=[("raw", i), "convw"], writes=[ka])
                        else:
                            tk.op("dve", lambda rg=rg, ag=ag, ci=ci, tap=tap: V.scalar_tensor_tensor(out=ag, in0=rg[:, :, tap:tap + 256], scalar=convw[:, ci, tap:tap + 1], in1=ag, op0=ALU.mult, op1=ALU.add),
                                  reads=[("raw", i), "convw", ka], writes=[ka])
                for g in range(NG):
                    cs = slice(g * 512, (g + 1) * 512)
                    ag = acc[:, 2 * g:2 * g + 2, :]
                    tk.op("act", lambda ag=ag, cs=cs, ci=ci: Sc.activation(out=qkvT[:, ci, cs].rearrange("p (s n) -> p s n", s=2), in_=ag, func=AF.Silu), reads=[("acc", g)], writes=[("qkvT", ci, g)])
            for ci in range(8):
                for g in range(NG):
                    cs = slice(g * 512, (g + 1) * 512)
                    ii = (ci * NG + g) % 2
                    kq = ("qkvT", ci, g)
                    b = psa.get()
                    tk.op("pool", lambda ii=ii, cs=cs, ci=ci: Pl.tensor_tensor(out=sqb[:, ii, :], in0=qkvT[:, ci, cs], in1=qkvT[:, ci, cs], op=ALU.mult), reads=[kq], writes=[("sqb", ii)])
                    tk.op("pe", lambda ii=ii, b=b: Pe.matmul(PS[:, b, :], lhsT=onesb[:], rhs=sqb[:, ii, :], start=True, stop=True), reads=[("sqb", ii), "onesb"], writes=[pk(b)])
                    sc_, eb_ = (128.0, epst[:, 1:2]) if ci < 4 else (1.0, epst[:, 0:1])
                    tk.op("act", lambda b=b, ii=ii, sc_=sc_, eb_=eb_: Sc.activation(out=rs2[:, ii, :], in_=PS[:, b, :], func=AF.Ln, bias=eb_, scale=sc_), reads=[pk(b), "epst"], writes=[("rs2", ii)])
                    psa.put(b)
                    tk.op("act", lambda ii=ii: Sc.activation(out=rs2[:, ii, :], in_=rs2[:, ii, :], func=AF.Exp, scale=-0.5), reads=[("rs2", ii)], writes=[("rs2", ii)])
                    tk.op("dve", lambda cs=cs, ci=ci, ii=ii: V.tensor_tensor(out=qkvT[:, ci, cs], in0=qkvT[:, ci, cs], in1=rs2[:, ii, :], op=ALU.mult), reads=[kq, ("rs2", ii)], writes=[kq])
            dump("qkvT%d" % l, qkvT[:, :, :], [("qkvT", ci, g) for ci in range(12) for g in range(NG)])
            tk.barrier()

            ob = off_qkv
            ostore, ob = carve(ob, [NT, 512], BF16)
            Sf, ob = carve(ob, [2, 512], F32)
            Sb, ob = carve(ob, [2, 512], BF16)
            yaT, ob = carve(ob, [2 * 4, 512], BF16)
            BUF = []
            for d in range(2):
                Bd = {}
                Bd["Dm"], ob = carve(ob, [4, 128], F32)
                Bd["Em"], ob = carve(ob, [4, 128], BF16)
                Bd["Rm"], ob = carve(ob, [4, 256], BF16)
                Bd["LT"], ob = carve(ob, [4, 128], BF16)
                Bd["Lm"], ob = carve(ob, [4, 128], BF16)
                Bd["Tm"], ob = carve(ob, [2 * 4, 128], BF16)
                Bd["TTm"], ob = carve(ob, [2 * 4, 128], BF16)
                Bd["egr"], ob = carve(ob, [4, 128], BF16)
                Bd["sm"], ob = carve(ob, [2, 32], F32)
                Bd["AT"], ob = carve(ob, [4, 128], BF16)
                Bd["kd"], ob = carve(ob, [4, 128], BF16)
                Bd["UW"], ob = carve(ob, [4, 256], BF16)
                Bd["wq"], ob = carve(ob, [2 * 4, 128], BF16)
                Bd["wT"] = Bd["wq"][:, 0:4, :]
                Bd["qdT"] = Bd["wq"][:, 4:8, :]
                Bd["vnew"], ob = carve(ob, [4, 128], BF16)
                Bd["gss"], ob = carve(ob, [1, 8], F32)
                Bd["yat"] = Bd["vnew"]
                BUF.append(Bd)
            woA_res = (ob + 4 * D * 2 <= NAR * 2)
            if woA_res:
                woA, ob = carve(ob, [4, D], BF16)
                wload(woA[:], "woA", w_out[l, 0:512, :].rearrange("(kc p) n -> p kc n", p=128))
            else:
                woS, ob = carve(ob, [2 * 4, 128], BF16)
            for d in range(2):
                tk.dma("sp", lambda d=d, l=l: nc.sync.dma_start(out=Sf[:, d, :].rearrange("p (h v) -> p h v", h=4), in_=s0in[l, d]), writes=[("Sf", d, h_) for h_ in range(4)])
                tk.op("act", lambda d=d: Sc.copy(out=Sb[:, d, :], in_=Sf[:, d, :]), reads=[("Sf", d, h_) for h_ in range(4)], writes=[("Sb", d)])
            gated = [0] * NG
            arrived = [0] * NT

            def pre(t, d):
                Bd = BUF[d]
                K = lambda n: (n, d)
                Dm, Em, Rm, LT, Lm, Tm, TTm, egr = Bd["Dm"], Bd["Em"], Bd["Rm"], Bd["LT"], Bd["Lm"], Bd["Tm"], Bd["TTm"], Bd["egr"]
                ATm, kdm, UWm, wTm, qdT = Bd["AT"], Bd["kd"], Bd["UW"], Bd["wT"], Bd["qdT"]
                Ebm = Lm[:, :, :]
                Ym = Dm[:, :, :].rearrange("p a n -> p (a n)").bitcast(BF16)[:, 0:512].rearrange("p (a n) -> p a n", a=4)
                tcols = slice(t * 128, (t + 1) * 128)
                hd = slice(d * 4, d * 4 + 4)
                sm = Bd["sm"][:, t % 2, :]
                KS = ("sm", d, t % 2)
                tri = cm[:, TRI_F + d, :]
                bkv = psa.get()
                ktok = PS[:, bkv, 0:256].bitcast(BF16).rearrange("p (h n) -> p h n", h=4)
                vtok = PS[:, bkv, 256:512].bitcast(BF16).rearrange("p (h n) -> p h n", h=4)
                tk.op("pe", lambda: [Pe.transpose(out=ktok[:, h, :], in_=qkvT[:, 4 + h, tcols], identity=identb) for h in range(4)]
                      + [Pe.transpose(out=vtok[:, h, :], in_=qkvT[:, 8 + h, tcols], identity=identb) for h in range(4)],
                      reads=[("qkvT", c) for c in range(4, 12)] + ["cmb"], writes=[pk(bkv)])
                bs_ = psa.get()
                tk.op("pe", lambda: [Pe.matmul(PS[:, bs_, 0:4], lhsT=tri, rhs=gl[:, t, hd], start=True, stop=True),
                                     Pe.matmul(PS[:, bs_, 4:8], lhsT=cm[:, BLK, :], rhs=gl[:, t, hd], start=True, stop=True)],
                      reads=["cm", "gl"], writes=[pk(bs_)])
                tk.op("dve", lambda: V.tensor_copy(out=sm[:, 0:8], in_=PS[:, bs_, 0:8]), reads=[pk(bs_)], writes=[KS])
                psa.put(bs_)
                tk.op("dve", lambda: V.tensor_copy(out=Dm[:, :, :], in_=bcast(gl[:, t, hd], 1, 128)), reads=["gl"], writes=[K("Dm")])
                yield
                bg = psa.get()
                gcrow = PS[:, bg, :].rearrange("p (h n) -> p h n", h=4)
                tk.op("pe", lambda: [Pe.matmul(gcrow[:, h, :], lhsT=Dm[:, h, :], rhs=tri, start=True, stop=True) for h in range(4)], reads=[K("Dm"), "cm"], writes=[pk(bg)])
                yield
                tk.op("dve", lambda: V.tensor_tensor(out=Dm[:, :, :], in0=gcrow, in1=bcast(sm[:, 0:4], 1, 128), op=ALU.subtract), reads=[pk(bg), KS], writes=[K("Dm")])
                tk.op("act", lambda: Sc.activation(out=egr[:, :, :], in_=gcrow, func=AF.Exp), reads=[pk(bg), K("Dm")], writes=[K("egr")])
                lastcol = (63, 127) if d == 0 else (0, 64)
                for c in range(2):
                    tk.op("act", lambda c=c: Sc.activation(out=sm[:, 16 + 4 * c:20 + 4 * c], in_=gcrow[:, :, lastcol[c]], func=AF.Exp), reads=[pk(bg)], writes=[KS])
                psa.put(bg)
                yield
                tk.op("pool", lambda: Pl.tensor_tensor(out=Dm[:, :, :], in0=Dm[:, :, :], in1=bcast(cm[:, NEG_F + d, :], 0, 4), op=ALU.add), reads=[K("Dm"), "cm"], writes=[K("Dm")])
                tk.op("dve", lambda: V.tensor_tensor(out=sm[:, 12:16], in0=sm[:, 4:8], in1=sm[:, 0:4], op=ALU.subtract), reads=[KS], writes=[KS])
                yield
                tk.op("act", lambda: Sc.activation(out=Em[:, :, :], in_=Dm[:, :, :], func=AF.Exp), reads=[K("Dm")], writes=[K("Em")])
                tk.op("act", lambda: Sc.activation(out=sm[:, 8:12], in_=sm[:, 0:4], func=AF.Exp), reads=[KS], writes=[KS])
                tk.op("act", lambda: Sc.activation(out=sm[:, 12:16], in_=sm[:, 12:16], func=AF.Exp), reads=[KS], writes=[KS])
                bkk = psa.get()
                kk = PS[:, bkk, :].rearrange("p (h n) -> p h n", h=4)
                tk.op("pe", lambda: [Pe.matmul(kk[:, h, :], lhsT=qkvT[:, 4 + h, tcols], rhs=qkvT[:, 4 + h, tcols], start=True, stop=True) for h in range(4)],
                      reads=[("qkvT", c) for c in range(4, 8)], writes=[pk(bkk)])
                tk.op("act", lambda: Sc.copy(out=Rm[:, :, 0:128], in_=vtok), reads=[pk(bkv)], writes=[K("Rm")])
                tk.op("dve", lambda: V.tensor_tensor(out=Rm[:, :, 128:256], in0=ktok, in1=bcast(sm[:, 8:12], 1, 128), op=ALU.mult), reads=[pk(bkv), KS], writes=[K("Rm")])
                psa.put(bkv)
                yield
                strict = cmb[:, STR_F + d, :]
                tk.op("pool", lambda: Pl.tensor_tensor(out=Ebm, in0=Em[:, :, :], in1=bcast(strict, 0, 4), op=ALU.mult), reads=[K("Em"), "cmb"], writes=[K("Lm")])
                yield
                tk.op("pool", lambda: Pl.tensor_tensor(out=Ebm, in0=Ebm, in1=bcast(beta[:, t, hd], 1, 128), op=ALU.mult), reads=[K("Lm"), "beta"], writes=[K("Lm")])
                yield
                tk.op("dve", lambda: V.tensor_tensor(out=LT[:, :, :], in0=kk, in1=Ebm, op=ALU.mult), reads=[pk(bkk), K("Lm")], writes=[K("LT")])
                psa.put(bkk)
                yield
                cur = 0
                tk.op("pool", lambda: Pl.tensor_tensor(out=Lm[:, :, :], in0=LT[:, :, :], in1=bcast(cmb[:, LM0, :], 0, 4), op=ALU.mult), reads=[K("LT"), "cmb"], writes=[K("Lm")])
                tk.op("pool", lambda: Pl.tensor_tensor(out=TTm[:, 0:4, :], in0=bcast(identb, 0, 4), in1=Lm[:, :, :], op=ALU.subtract), reads=[K("Lm"), "cmb"], writes=[K("TT0")])
                yield
                bt = psa.get()
                tps = PS[:, bt, 0:256].bitcast(BF16).rearrange("p (h n) -> p h n", h=4)
                tk.op("pe", lambda: [Pe.transpose(out=tps[:, h, :], in_=TTm[:, h, :], identity=identb) for h in range(4)], reads=[K("TT0"), "cmb"], writes=[pk(bt)])
                yield
                tk.op("act", lambda: Sc.copy(out=Tm[:, 0:4, :], in_=tps), reads=[pk(bt)], writes=[K("T0")])
                psa.put(bt)
                for k in range(1, 6):
                    nx = 1 - cur
                    tk.op("pool", lambda k=k: Pl.tensor_tensor(out=Lm[:, :, :], in0=LT[:, :, :], in1=bcast(cmb[:, LM0 + k, :], 0, 4), op=ALU.mult), reads=[K("LT"), "cmb"], writes=[K("Lm")])
                    yield
                    by = psa.get()
                    yps = PS[:, by, :].rearrange("p (h n) -> p h n", h=4)
                    tk.op("pe", lambda cur=cur: [Pe.matmul(yps[:, h, :], lhsT=Lm[:, h, :], rhs=Tm[:, cur * 4 + h, :], start=True, stop=True) for h in range(4)], reads=[K("Lm"), K("T%d" % cur)], writes=[pk(by)])
                    yield
                    tk.op("act", lambda: Sc.mul(out=Ym[:, :, :], in_=yps, mul=-1.0), reads=[pk(by)], writes=[K("Dm")])
                    psa.put(by)
                    yield
                    bz = psa.get()
                    zps = PS[:, bz, :].rearrange("p (h n) -> p h n", h=4)

                    def mz(cur=cur, zps=zps):
                        r = []
                        for h in range(4):
                            r.append(Pe.matmul(zps[:, h, :], lhsT=identb, rhs=TTm[:, cur * 4 + h, :], start=True, stop=False))
                            r.append(Pe.matmul(zps[:, h, :], lhsT=Ym[:, h, :], rhs=TTm[:, cur * 4 + h, :], start=False, stop=True))
                        return r
                    tk.op("pe", mz, reads=[K("Dm"), K("TT%d" % cur), "cmb"], writes=[pk(bz)])
                    bz2 = None
                    if k < 5:
                        bz2 = psa.get()
                        zps2 = PS[:, bz2, :].rearrange("p (h n) -> p h n", h=4)

                        def mz2(cur=cur, zps2=zps2):
                            r = []
                            for h in range(4):
                                r.append(Pe.matmul(zps2[:, h, :], lhsT=identb, rhs=Tm[:, cur * 4 + h, :], start=True, stop=False))
                                r.append(Pe.matmul(zps2[:, h, :], lhsT=TTm[:, cur * 4 + h, :], rhs=Ym[:, h, :], start=False, stop=True))
                            return r
                        tk.op("pe", mz2, reads=[K("Dm"), K("TT%d" % cur), K("T%d" % cur), "cmb"], writes=[pk(bz2)])
                    yield
                    tk.op("act", lambda nx=nx: Sc.copy(out=TTm[:, nx * 4:nx * 4 + 4, :], in_=zps), reads=[pk(bz)], writes=[K("TT%d" % nx)])
                    psa.put(bz)
                    if k < 5:
                        tk.op("dve", lambda nx=nx: V.tensor_copy(out=Tm[:, nx * 4:nx * 4 + 4, :], in_=zps2), reads=[pk(bz2)], writes=[K("T%d" % nx)])
                        psa.put(bz2)
                    cur = nx
                    yield
                yield "BAR"
                bqk = psa.get()
                qk = PS[:, bqk, :].rearrange("p (h n) -> p h n", h=4)
                tk.op("pe", lambda: [Pe.matmul(qk[:, h, :], lhsT=qkvT[:, 4 + h, tcols], rhs=qkvT[:, h, tcols], start=True, stop=True) for h in range(4)],
                      reads=[("qkvT", c) for c in range(0, 8)], writes=[pk(bqk)])
                bk2 = psa.get()
                ktok2 = PS[:, bk2, 0:256].bitcast(BF16).rearrange("p (h n) -> p h n", h=4)
                tk.op("pe", lambda: [Pe.transpose(out=ktok2[:, h, :], in_=qkvT[:, 4 + h, tcols], identity=identb) for h in range(4)],
                      reads=[("qkvT", c) for c in range(4, 8)] + ["cmb"], writes=[pk(bk2)])
                yield
                tk.op("dve", lambda: V.tensor_tensor(out=ATm[:, :, :], in0=qk, in1=Em[:, :, :], op=ALU.mult), reads=[pk(bqk), K("Em")], writes=[K("AT")])
                psa.put(bqk)
                tk.op("dve", lambda: V.tensor_tensor(out=kdm[:, :, :], in0=ktok2, in1=bcast(sm[:, 12:16], 1, 128), op=ALU.mult), reads=[pk(bk2), KS], writes=[K("kd")])
                psa.put(bk2)
                yield
                for half in range(2):
                    bu = psa.get()
                    uw = PS[:, bu, :].rearrange("p (h n) -> p h n", h=2)
                    tk.op("pe", lambda cur=cur, half=half, uw=uw: [Pe.matmul(uw[:, hh, :], lhsT=TTm[:, cur * 4 + half * 2 + hh, :], rhs=Rm[:, half * 2 + hh, :], start=True, stop=True) for hh in range(2)],
                          reads=[K("Rm"), K("TT%d" % cur)], writes=[pk(bu)])
                    yield
                    tk.op("dve", lambda half=half, uw=uw: V.tensor_tensor(out=UWm[:, half * 2:half * 2 + 2, :], in0=uw, in1=bcast(beta[:, t, d * 4 + half * 2:d * 4 + half * 2 + 2], 1, 256), op=ALU.mult),
                          reads=[pk(bu), "beta"], writes=[K("UW")])
                    psa.put(bu)
                yield
                bw = psa.get()
                wps = PS[:, bw, 0:256].bitcast(BF16).rearrange("p (h n) -> p h n", h=4)
                tk.op("pe", lambda: [Pe.transpose(out=wps[:, h, :], in_=UWm[:, h, 128:256], identity=identb) for h in range(4)], reads=[K("UW"), "cmb"], writes=[pk(bw)])
                tk.op("pool", lambda: Pl.tensor_tensor(out=qdT[:, :, :], in0=qkvT[:, 0:4, tcols], in1=egr[:, :, :], op=ALU.mult), reads=[("qkvT", c) for c in range(4)] + [K("egr")], writes=[K("qdT")])
                yield
                tk.op("act", lambda: Sc.copy(out=wTm[:, :, :], in_=wps), reads=[pk(bw)], writes=[K("wT")])
                psa.put(bw)
                yield

            def scan(t, d):
                Bd = BUF[d]
                K = lambda n: (n, d)
                ATm, kdm, UWm, wTm, qdT, vnew = Bd["AT"], Bd["kd"], Bd["UW"], Bd["wT"], Bd["qdT"], Bd["vnew"]
                sm = Bd["sm"][:, t % 2, :]
                KS = ("sm", d, t % 2)
                bo = psa.get()
                ops = PS[:, bo, :].rearrange("p (h n) -> p h n", h=4)
                order = (0, 1) if d == 0 else (1, 0)
                for c in order:
                    pr = slice(c * 64, c * 64 + 64)
                    bw_ = psa.get()
                    wsps = PS[pr, bw_, :].rearrange("p (h n) -> p h n", h=4)
                    tk.op("pe", lambda: [Pe.matmul(wsps[:, h, :], lhsT=wTm[:, h, pr], rhs=Sb[:, d, h * 128:(h + 1) * 128], start=True, stop=True) for h in range(4)],
                          reads=[K("wT"), ("Sb", d)], writes=[pk(bw_)])
                    yield
                    tk.op("dve", lambda: V.tensor_tensor(out=vnew[pr, 0:4, :], in0=UWm[pr, 0:4, 0:128], in1=wsps, op=ALU.subtract), reads=[pk(bw_), K("UW")], writes=[K("vnew")])
                    psa.put(bw_)
                    yield

                    def omm():
                        r = []
                        for h in range(4):
                            r.append(Pe.matmul(ops[pr, h, :], lhsT=qdT[:, h, pr], rhs=Sb[:, d, h * 128:(h + 1) * 128], start=True, stop=False))
                            r.append(Pe.matmul(ops[pr, h, :], lhsT=ATm[pr, h, pr], rhs=vnew[pr, h, :], start=False, stop=True))
                        return r
                    bd = psa.get()
                    dps = PS[:, bd, :].rearrange("p (h n) -> p h n", h=4)
                    tk.op("pe", lambda: [Pe.matmul(dps[:, h, :], lhsT=kdm[pr, h, :], rhs=vnew[pr, h, :], start=True, stop=True) for h in range(4)],
                          reads=[K("kd"), K("vnew")], writes=[pk(bd)])
                    tk.op("pe", omm, reads=[K("qdT"), ("Sb", d), K("AT"), K("vnew")], writes=[pk(bo)])
                    yield
                    for h in range(4):
                        tk.op("dve", lambda h=h: V.scalar_tensor_tensor(out=Sf[:, d, h * 128:(h + 1) * 128], in0=Sf[:, d, h * 128:(h + 1) * 128], scalar=sm[:, 16 + 4 * c + h:17 + 4 * c + h],
                                                                         in1=dps[:, h, :], op0=ALU.mult, op1=ALU.add), reads=[pk(bd), ("Sf", d, h)], sreads=[KS], writes=[("Sf", d, h)])
                    psa.put(bd)
                    yield
                    tk.op("act", lambda: Sc.copy(out=Sb[:, d, :], in_=Sf[:, d, :]), reads=[("Sf", d, h_) for h_ in range(4)], writes=[("Sb", d)])
                    yield
                arrived[t] += 1
                if arrived[t] == 1:
                    tk.op("act", lambda: Sc.copy(out=ostore[:, t, :].rearrange("p (h n) -> p h n", h=4), in_=ops), reads=[pk(bo)], writes=[("ost", t)])
                    psa.put(bo)
                    yield
                else:
                    ot = Bd["UW"][:, :, :].rearrange("p a n -> p (a n)").bitcast(F32).rearrange("p (a n) -> p a n", a=4)
                    osq = Bd["wq"][:, :, :].rearrange("p a n -> p (a n)").bitcast(F32).rearrange("p (a n) -> p a n", a=4)
                    gss, yat = Bd["gss"], Bd["yat"]
                    tk.op("dve", lambda: V.tensor_tensor(out=ot[:, :, :], in0=ops, in1=ostore[:, t, :].rearrange("p (h n) -> p h n", h=4), op=ALU.add), reads=[pk(bo), ("ost", t)], writes=[K("UW")])
                    psa.put(bo)
                    yield
                    tk.op("pool", lambda: Pl.tensor_tensor(out=osq, in0=ot[:, :, :], in1=ot[:, :, :], op=ALU.mult), reads=[K("UW")], writes=[K("wT"), K("qdT")])
                    yield
                    tk.op("dve", lambda: V.tensor_reduce(out=gss[:, 0, 0:4], in_=osq, axis=mybir.AxisListType.X, op=ALU.add), reads=[K("wT"), K("qdT")], writes=[K("gss")])
                    tk.op("act", lambda: Sc.activation(out=gss[:, 0, 0:4], in_=gss[:, 0, 0:4], func=AF.Ln, bias=epst[:, 0:1], scale=1.0 / 128), reads=[K("gss"), "epst"], writes=[K("gss")])
                    tk.op("act", lambda: Sc.activation(out=gss[:, 0, 4:8], in_=gss[:, 0, 0:4], func=AF.Exp, scale=-0.5), reads=[K("gss")], writes=[K("gss")])
                    yield
                    for h in range(4):
                        tk.op("dve", lambda h=h: V.scalar_tensor_tensor(out=yat[:, h, :], in0=ot[:, h, :], scalar=gss[:, 0, 4 + h:5 + h], in1=sz[:, t, h * 128:(h + 1) * 128], op0=ALU.mult, op1=ALU.mult),
                              reads=[K("UW"), ("sz", t)], sreads=[K("gss")], writes=[K("vnew")])
                    yield
                    g = t // 4
                    gi = g % 2
                    by = psa.get()
                    yps = PS[:, by, 0:256].bitcast(BF16).rearrange("p (h n) -> p h n", h=4)
                    tk.op("pe", lambda: [Pe.transpose(out=yps[:, h, :], in_=yat[:, h, :], identity=identb) for h in range(4)], reads=[K("vnew"), "cmb"], writes=[pk(by)])
                    yield
                    tk.op("act", lambda: Sc.copy(out=yaT[:, gi * 4:gi * 4 + 4, (t % 4) * 128:(t % 4) * 128 + 128], in_=yps), reads=[pk(by)], writes=[("yaT", gi)])
                    psa.put(by)
                    gated[g] += 1
                    do_out = (gated[g] == 4)
                    yield
                    if do_out:
                        cols = slice(g * 512, (g + 1) * 512)
                        if g == 0:
                            dump("yaT%d" % l, yaT[:, gi * 4:gi * 4 + 4, :], [("yaT", gi)])
                        for dc in range(DC):
                            if woA_res:
                                wsl = lambda kc, dc=dc: woA[:, kc, dc * 128:(dc + 1) * 128]
                                wkeys = ["woA"]
                            else:
                                jj = dc % 2
                                wload(woS[:, jj * 4:jj * 4 + 4, :], ("woS", jj), w_out[l, 0:512, dc * 128:(dc + 1) * 128].rearrange("(kc p) n -> p kc n", p=128))
                                wsl = lambda kc, jj=jj: woS[:, jj * 4 + kc, :]
                                wkeys = [("woS", jj)]
                            b = psa.get()
                            tk.op("pe", lambda b=b, wsl=wsl: [Pe.matmul(PS[:, b, :], lhsT=wsl(kc), rhs=yaT[:, gi * 4 + kc, :], start=(kc == 0), stop=(kc == 3)) for kc in range(4)],
                                  reads=wkeys + [("yaT", gi)], writes=[pk(b)])
                            yield
                            tk.op("dve", lambda b=b, dc=dc: V.scalar_tensor_tensor(out=xT[:, dc, cols], in0=PS[:, b, :], scalar=g1v[:, dc:dc + 1], in1=xT[:, dc, cols], op0=ALU.mult, op1=ALU.add),
                                  reads=[pk(b), "modv", ("xT", g)], writes=[("xT", g)])
                            psa.put(b)
                            yield

            def slot_end(t, d):
                slot = t // 2
                tk.dma("sp", lambda: nc.sync.dma_start(out=stout[slot, l, d].rearrange("h k v -> k h v"), in_=Sf[:, d, :].rearrange("p (h v) -> p h v", h=4)), reads=[("Sf", d, h_) for h_ in range(4)])
                tk.op("dve", lambda: V.tensor_scalar(out=Sf[:, d, :], in0=Sf[:, d, :], scalar1=chain[:, 0:1], scalar2=None, op0=ALU.mult), reads=[("Sf", d, h_) for h_ in range(4)] + ["chain"], writes=[("Sf", d, h_) for h_ in range(4)])
                tk.op("act", lambda: Sc.copy(out=Sb[:, d, :], in_=Sf[:, d, :]), reads=[("Sf", d, h_) for h_ in range(4)], writes=[("Sb", d)])

            def scan_full(t, d):
                yield from scan(t, d)
                if (d == 0 and t % 2 == 1) or (d == 1 and t % 2 == 0):
                    slot_end(t, d)
                    yield

            def stream(d):
                tiles = list(range(NT)) if d == 0 else list(range(NT - 1, -1, -1))
                for _ in pre(tiles[0], d):
                    yield
                for idx, t in enumerate(tiles):
                    gs = scan_full(t, d)
                    gp = pre(tiles[idx + 1], d) if (idx + 1 < NT and OVERLAP) else None
                    s_alive, p_alive, p_bar = True, gp is not None, False
                    while s_alive or p_alive:
                        if s_alive:
                            try:
                                next(gs)
                                yield
                            except StopIteration:
                                s_alive = False
                        if p_alive and not (p_bar and s_alive):
                            try:
                                tok = next(gp)
                                if tok == "BAR" and s_alive:
                                    p_bar = True
                                yield
                            except StopIteration:
                                p_alive = False
                    if gp is None and idx + 1 < NT:
                        for _ in pre(tiles[idx + 1], d):
                            yield

            gens = [stream(0), stream(1)]
            alive = [True, True]
            if SKIPB:
                gens = []
                alive = [False, False]
            if not INTERLEAVE:
                for gg_ in gens:
                    for n_, _ in enumerate(gg_):
                        if n_ >= BLIMIT:
                            break
                alive = [False, False]
            while any(alive):
                for gi_ in range(2):
                    if alive[gi_]:
                        try:
                            next(gens[gi_])
                        except StopIteration:
                            alive[gi_] = False
            dump("xmid%d" % l, xT[:, :, :], [("xT", g) for g in range(NG)])
            tk.barrier()

            od = 0
            h2T, od = carve(od, [DC, HT], BF16)
            actT, od = carve(od, [FC, HT], BF16)
            wdn, od = carve(od, [FC, D], BF16)
            wgu, od = carve(od, [3 * 2 * 8, 128], BF16)
            sqb, od = carve(od, [2, 512], BF16)
            rs, od = carve(od, [1, 512], F32)
            rinv, od = carve(od, [1, 512], F32)
            tmp, od = carve(od, [4, 512], F32)
            sg, od = carve(od, [2, 512], BF16)
            w_gu_l = w_gu[l].rearrange("(kc p) n -> p kc n", p=128)
            for fq in range(0, FC, 2):
                n = min(2, FC - fq)
                wload(wdn[:, fq:fq + n, :], ("wdn", fq // 2), w_down[l, fq * 128:(fq + n) * 128, :].rearrange("(f p) n -> p f n", p=128))

            def wgu_load(f):
                j = f % 3
                wload(wgu[:, j * 16:j * 16 + 8, :], ("wgu", j, 0), w_gu_l[:, :, f * 128:(f + 1) * 128])
                wload(wgu[:, j * 16 + 8:j * 16 + 16, :], ("wgu", j, 1), w_gu_l[:, :, FF + f * 128:FF + (f + 1) * 128])
            def ffn_norm(hf):
                for gg in range(NGH):
                    g = hf * NGH + gg
                    yield from norm_mod_gen(slice(g * 512, (g + 1) * 512), gm2[:, l, :], sh2, h2T, slice(gg * 512, (gg + 1) * 512), ("h2T", gg), sqb, rs, rinv, tmp, [("xT", g)])

            def ffn_down(hf):
                for gg in range(NGH):
                    g = hf * NGH + gg
                    hc = slice(gg * 512, (gg + 1) * 512)
                    cols = slice(g * 512, (g + 1) * 512)
                    for dc in range(DC):
                        b = psa.get()
                        tk.op("pe", lambda b=b, dc=dc, hc=hc: [Pe.matmul(PS[:, b, :], lhsT=wdn[:, f, dc * 128:(dc + 1) * 128], rhs=actT[:, f, hc], start=(f == 0), stop=(f == FC - 1)) for f in range(FC)],
                              reads=[("wdn", q) for q in range(FC // 2)] + [("actT", gg)], writes=[pk(b)])
                        tk.op("dve", lambda b=b, dc=dc, cols=cols: V.scalar_tensor_tensor(out=xT[:, dc, cols], in0=PS[:, b, :], scalar=g2v[:, dc:dc + 1], in1=xT[:, dc, cols], op0=ALU.mult, op1=ALU.add),
                              reads=[pk(b), "modv", ("xT", g)], writes=[("xT", g)])
                        psa.put(b)
                        yield

            for hf in range(NHALF):
                if hf == 0:
                    for _ in ffn_norm(0):
                        pass
                wgu_load(0)
                wgu_load(1)
                for f in range(FC):
                    j = f % 3
                    if f + 2 < FC:
                        wgu_load(f + 2)
                    for gg in range(NGH):
                        hc = slice(gg * 512, (gg + 1) * 512)
                        i = (f * NGH + gg) % 2
                        b1 = psa.get()
                        b2 = psa.get()
                        tk.op("pe", lambda b1=b1, hc=hc, j=j: [Pe.matmul(PS[:, b1, :], lhsT=wgu[:, j * 16 + kc, :], rhs=h2T[:, kc, hc], start=(kc == 0), stop=(kc == 7)) for kc in range(8)],
                              reads=[("wgu", j, 0), ("h2T", gg)], writes=[pk(b1)])
                        tk.op("pe", lambda b2=b2, hc=hc, j=j: [Pe.matmul(PS[:, b2, :], lhsT=wgu[:, j * 16 + 8 + kc, :], rhs=h2T[:, kc, hc], start=(kc == 0), stop=(kc == 7)) for kc in range(8)],
                              reads=[("wgu", j, 1), ("h2T", gg)], writes=[pk(b2)])
                        tk.op("act", lambda b1=b1, i=i: Sc.activation(out=sg[:, i, :], in_=PS[:, b1, :], func=AF.Silu), reads=[pk(b1)], writes=[("sg", i)])
                        tk.op("dve", lambda b2=b2, i=i, f=f, hc=hc: V.tensor_tensor(out=actT[:, f, hc], in0=PS[:, b2, :], in1=sg[:, i, :], op=ALU.mult), reads=[pk(b2), ("sg", i)], writes=[("actT", gg)])
                        psa.put(b1)
                        psa.put(b2)
                interleave(ffn_down(hf), ffn_norm(hf + 1) if hf + 1 < NHALF else None)
            dump("xend%d" % l, xT[:, :, :], [("xT", g) for g in range(NG)])
            tk.barrier()

        of = 0
        sqb, of = carve(of, [2, 512], BF16)
        rs, of = carve(of, [1, 512], F32)
        rinv, of = carve(of, [1, 512], F32)
        yfin, of = carve(of, [DC, 512], F32)
        ytok, of = carve(of, [2, D], F32)
        for g in range(NG):
            cols = slice(g * 512, (g + 1) * 512)
            rms_rinv(cols, 1.0 / D, sqb, rs, rinv, [("xT", g)])
            for c in range(DC):
                tk.op("dve", lambda c=c: V.scalar_tensor_tensor(out=yfin[:, c, :], in0=xT[:, c, cols], scalar=nfg_sb[:, c:c + 1], in1=rinv[:, 0, :], op0=ALU.mult, op1=ALU.mult),
                      reads=[("xT", g), "rinv", "nfg"], writes=["yfin"])
            for tt in range(4):
                t = g * 4 + tt
                i = t % 2
                for hh in range(2):
                    b = psa.get()
                    tk.op("pe", lambda b=b, hh=hh, tt=tt: [Pe.transpose(out=PS[:, b, j * 128:(j + 1) * 128], in_=yfin[:, hh * 4 + j, tt * 128:(tt + 1) * 128], identity=identf) for j in range(4)],
                          reads=["yfin", "cm"], writes=[pk(b)])
                    if hh == 0:
                        tk.op("act", lambda b=b, i=i: Sc.copy(out=ytok[:, i, 0:512], in_=PS[:, b, :]), reads=[pk(b)], writes=[("ytok", i)])
                    else:
                        tk.op("dve", lambda b=b, i=i: V.tensor_copy(out=ytok[:, i, 512:1024], in_=PS[:, b, :]), reads=[pk(b)], writes=[("ytok", i)])
                    psa.put(b)
                tk.dma("sp", lambda t=t, i=i: nc.sync.dma_start(out=yout[t * 128:(t + 1) * 128, :], in_=ytok[:, i, :]), reads=[("ytok", i)])
        for e in ("sp", "act", "pool", "dve", "pe"):
            tk.wait_all(e)
        build.stats = (tk.nins, tk.nwait, tk.nsem)
    return nc


def _masks():
    j = np.arange(128)[:, None]
    i = np.arange(128)[None, :]
    same = (j // 64) == (i // 64)
    f = np.zeros((6, 128, 128), np.float32)
    f[0] = (j == i)
    f[1] = same & (j <= i)
    f[2] = same & (j >= i)
    f[3] = same
    f[4] = np.where(same & (i >= j), 0.0, NEG)
    f[5] = np.where(same & (i <= j), 0.0, NEG)
    b = np.zeros((9, 128, 128), np.float32)
    b[0] = (j == i)
    b[1] = same & (i > j)
    b[2] = same & (i < j)
    for k in range(6):
        bs = 1 << k
        b[3 + k] = ((j // (2 * bs)) == (i // (2 * bs))) & ((j // bs) != (i // bs))
    return (np.ascontiguousarray(f.transpose(1, 0, 2)), np.ascontiguousarray(b.transpose(1, 0, 2)))


def _pool_mats(seg):
    wins = (2, 4, 8, 16)
    P = np.zeros((4, 4, 128, 128), np.float64)
    for wi, win in enumerate(wins):
        for par in range(2):
            for to in range(128):
                pos_slot = par * 128 + to
                s0 = (pos_slot // seg) * seg
                pos = pos_slot - s0
                lo = min(max(pos - win // 2, 0), seg)
                hi = min(max(pos + win - win // 2, 0), seg)
                cnt = hi - lo
                for p in range(lo, hi):
                    tin_slot = s0 + p
                    tpar, tin = tin_slot // 128, tin_slot % 128
                    if tpar == par:
                        P[par, wi, tin, to] += 1.0 / cnt
                    else:
                        P[2 + par, wi, tin, to] += 1.0 / cnt
            P[par, wi] -= np.eye(128)
    return np.ascontiguousarray(P.transpose(2, 0, 1, 3)).astype(np.float32)


def shared_inputs(w_in, conv_w, a_log, dt_bias, delta_norm_g, sgu_norm_g, w_spatial, b_spatial, w_pool, pool_scale,
                  w_out, norm1_g, norm2_g, w_mod, b_mod, w_gu, w_down, norm_f):
    f = lambda a: np.ascontiguousarray(np.asarray(a, dtype=np.float32))
    NL = w_in.shape[0]
    cmf, cmb = _masks()
    pv = lambda v, n: f(np.asarray(v).reshape(NL, n, 128).transpose(0, 2, 1))
    wp = np.asarray(w_pool)
    wpl = wp.reshape(NL, 2, 2, 64, 64).transpose(0, 2, 3, 1, 4).reshape(NL, 128, 2, 64)
    return {
        "cmf": cmf, "cmb": cmb,
        "omega": np.ascontiguousarray((1.0 / (10000.0 ** (np.arange(256, dtype=np.float32) / np.float32(256)))).astype(np.float32).reshape(2, 128).T),
        "w_mod": f(w_mod), "b_mod": pv(b_mod, 48), "w_in": f(w_in),
        "conv_w": f(np.asarray(conv_w).reshape(NL, 3, 12, 128).transpose(0, 3, 2, 1)),
        "a_log": f(np.broadcast_to(np.asarray(a_log).reshape(NL, 1, 8), (NL, 128, 8))),
        "dt_bias": f(np.broadcast_to(np.asarray(dt_bias).reshape(NL, 1, 8), (NL, 128, 8))),
        "dng": f(np.broadcast_to(np.tile(np.asarray(delta_norm_g), (1, 4)).reshape(NL, 1, 512), (NL, 128, 512))),
        "sng": f(np.broadcast_to(np.asarray(sgu_norm_g).reshape(NL, 1, 256), (NL, 128, 256))),
        "wsT": f(np.asarray(w_spatial).transpose(0, 3, 1, 2)),
        "bsp": f(np.asarray(b_spatial).reshape(NL, 1, 512)),
        "wpool": f(wpl), "pscale": pv(pool_scale, 2),
        "w_out": f(w_out), "n1g": pv(norm1_g, 8), "n2g": pv(norm2_g, 8),
        "nfg": f(np.asarray(norm_f).reshape(8, 128).T),
        "w_gu": f(w_gu), "w_down": f(w_down),
    }


def core_inputs(x_tok, pe, cvec, s0, chain, seg):
    T = x_tok.shape[0]
    return {
        "xin": np.ascontiguousarray(x_tok, dtype=np.float32),
        "cvec": np.ascontiguousarray(np.asarray(cvec, dtype=np.float32).reshape(8, 128).T),
        "s0in": np.ascontiguousarray(np.asarray(s0, dtype=np.float32).transpose(0, 1, 3, 2, 4)),
        "chain": np.full((128, 1), chain, np.float32),
        "poolm": _pool_mats(seg),
    }


_NC_CACHE = {}


def run(x_prompt, x_sample, state_delta, c, c_ctx, w_in, conv_w, a_log, dt_bias,
        delta_norm_g, sgu_norm_g, w_spatial, b_spatial, w_pool, pool_scale, w_out,
        norm1_g, norm2_g, w_mod, b_mod, w_gu, w_down, norm_f):
    x_prompt = np.asarray(x_prompt, np.float32)
    x_sample = np.asarray(x_sample, np.float32)
    state_delta = np.asarray(state_delta, np.float32)
    c = np.asarray(c, np.float32)
    B, S, _ = x_prompt.shape
    DB, DS, _ = x_sample.shape
    NL = w_in.shape[0]
    T = DS
    NS = T // 256
    shared = shared_inputs(w_in, conv_w, a_log, dt_bias, delta_norm_g, sgu_norm_g, w_spatial, b_spatial, w_pool,
                           pool_scale, w_out, norm1_g, norm2_g, w_mod, b_mod, w_gu, w_down, norm_f)
    n_ctx = 8 - DB
    base, extra = divmod(B, n_ctx)
    counts = [base + (1 if i < extra else 0) for i in range(n_ctx)]
    assert max(counts) <= NS
    zero_s = np.zeros((NL, 2, 4, 128, 128), np.float32)
    in_maps, seq_of = [], []
    start = 0
    for i in range(n_ctx):
        xt = np.zeros((T, D), np.float32)
        n = counts[i]
        xt[:n * S] = x_prompt[start:start + n].reshape(n * S, D)
        seq_of.append((start, n))
        start += n
        m = core_inputs(xt, None, c_ctx, zero_s, 0.0, 256)
        m.update(shared)
        in_maps.append(m)
    for b in range(DB):
        m = core_inputs(x_sample[b], None, c[b], state_delta[b], 1.0, 64)
        m.update(shared)
        in_maps.append(m)
    key = (T, NL)
    if key not in _NC_CACHE:
        _NC_CACHE[key] = build(T, NL)
    res = run_bass_kernel_spmd(_NC_CACHE[key], in_maps, core_ids=list(range(8)))
    y_prompt = np.zeros((B, S, D), np.float32)
    new_state = np.zeros((B, NL, 2, 4, 128, 128), np.float32)
    for i in range(n_ctx):
        s0_, n = seq_of[i]
        r = res.results[i]
        y_prompt[s0_:s0_ + n] = np.asarray(r["yout"])[:n * S].reshape(n, S, D)
        new_state[s0_:s0_ + n] = np.asarray(r["stout"])[:n]
    y_sample = np.stack([np.asarray(res.results[n_ctx + b]["yout"]) for b in range(DB)], axis=0)
    return (y_prompt, y_sample, new_state)


def kernel(x_prompt, x_sample, state_delta, c, c_ctx, w_in, conv_w, a_log, dt_bias,
           delta_norm_g, sgu_norm_g, w_spatial, b_spatial, w_pool, pool_scale, w_out,
           norm1_g, norm2_g, w_mod, b_mod, w_gu, w_down, norm_f):
    return run(x_prompt, x_sample, state_delta, c, c_ctx, w_in, conv_w, a_log, dt_bias,
               delta_norm_g, sgu_norm_g, w_spatial, b_spatial, w_pool, pool_scale, w_out,
               norm1_g, norm2_g, w_mod, b_mod, w_gu, w_down, norm_f)
```
